# Optimizing a Trainium2 kernel written in Bass

```python
import math
import jax, jax.numpy as jnp
from jax import lax
import numpy as np

D_MODEL = 1024
BATCH = 4
SEQ = 8192
DEPTH = 2

CHUNK = 64
Q_BLOCK = 128
PLE_DIM = 256
MIX_WIDTH = D_MODEL // 2
POOL_WINDOWS = (2, 4, 8, 16)
POOL_GROUPS = len(POOL_WINDOWS)
POOL_GROUP_DIM = MIX_WIDTH // POOL_GROUPS
CONV_WIDTH = 3
ATTN_HEADS = 4
ATTN_HEAD_DIM = MIX_WIDTH // (2 * ATTN_HEADS)
ATTN_V_DIM = 2 * ATTN_HEAD_DIM
ATTN_QK_WIDTH = ATTN_HEADS * 2 * ATTN_HEAD_DIM
ATTN_V_WIDTH = ATTN_HEADS * ATTN_V_DIM
N_BRANCHES = 3
D_IN = 4 * MIX_WIDTH + 2 * ATTN_QK_WIDTH + ATTN_V_WIDTH + N_BRANCHES * D_MODEL
D_FF = 4 * D_MODEL
EPS = 1e-6
NEG_INF = -1e30

kernel_name = 'hybrid_pool_conv_diffattn_block'


def rms_norm(x, g):
    xf = x.astype(jnp.float32)
    y = xf * lax.rsqrt(jnp.mean(xf * xf, axis=-1, keepdims=True) + EPS)
    return (y * g.astype(jnp.float32)).astype(x.dtype)


def multiscale_pool(a, w, scale):
    b_, s_, _ = a.shape
    af = a.astype(jnp.float32).reshape(b_, s_, POOL_GROUPS, POOL_GROUP_DIM)
    cs = jnp.cumsum(af, axis=1)
    t = jnp.arange(s_)
    outs = []
    for g, win in enumerate(POOL_WINDOWS):
        csg = cs[:, :, g]
        prev = jnp.pad(csg, ((0, 0), (win, 0), (0, 0)))[:, :s_]
        cnt = jnp.minimum(t + 1, win).astype(jnp.float32)[None, :, None]
        outs.append((csg - prev) / cnt - af[:, :, g])
    pooled = jnp.stack(outs, axis=2)
    mixed = jnp.einsum('bsgc,gcd->bsgd', pooled, w.astype(jnp.float32))
    return (mixed.reshape(b_, s_, MIX_WIDTH) * scale.astype(jnp.float32)).astype(a.dtype)


def causal_depthwise_conv(z, w):
    c = z.shape[-1]
    return lax.conv_general_dilated(
        z, w[:, None, :].astype(z.dtype), window_strides=(1,),
        padding=[(CONV_WIDTH - 1, 0)], dimension_numbers=('NWC', 'WIO', 'NWC'),
        feature_group_count=c)


def diff_attention(q1, q2, k1, k2, v, lam):
    b_, s_, h_, d_ = q1.shape
    dv = v.shape[-1]
    nb = s_ // Q_BLOCK
    scale = d_ ** -0.5
    k1, k2, v = (t.transpose(0, 2, 1, 3) for t in (k1, k2, v))

    def to_blocks(q):
        return q.reshape(b_, nb, Q_BLOCK, h_, d_).transpose(1, 0, 3, 2, 4)

    key_chunk = jnp.arange(s_) // CHUNK

    def one_block(args):
        qb1, qb2, bi = args
        q_chunk = (bi * Q_BLOCK + jnp.arange(Q_BLOCK)) // CHUNK
        mask = key_chunk[None, :] <= q_chunk[:, None]

        def probs(q, k):
            s = jnp.einsum('bhqd,bhkd->bhqk', q, k).astype(jnp.float32) * scale
            return jax.nn.softmax(jnp.where(mask, s, NEG_INF), axis=-1)

        att = probs(qb1, k1) - lam * probs(qb2, k2)
        return jnp.einsum('bhqk,bhkd->bhqd', att.astype(v.dtype), v)

    out = lax.map(one_block, (to_blocks(q1), to_blocks(q2), jnp.arange(nb)))
    return out.transpose(1, 0, 3, 2, 4).reshape(b_, s_, h_, dv)


def setup_inputs(seed: int = 0) -> dict:
    key = jax.random.key(seed)
    ks = jax.random.split(key, 24)
    nrm = lambda k, shape, s: jax.random.normal(k, shape, jnp.float32) * s
    gain = lambda k, shape: 1.0 + 0.1 * jax.random.normal(k, shape, jnp.float32)
    L = DEPTH
    return {
        'x': nrm(ks[0], (BATCH, SEQ, D_MODEL), 1.0),
        'p': nrm(ks[1], (DEPTH, BATCH, SEQ, PLE_DIM), 1.0),
        'norm_mix_g': gain(ks[2], (L, D_MODEL)),
        'w_in': nrm(ks[3], (L, D_MODEL, D_IN), D_MODEL ** -0.5),
        'pool_w': nrm(ks[4], (L, POOL_GROUPS, POOL_GROUP_DIM, POOL_GROUP_DIM), POOL_GROUP_DIM ** -0.5),
        'pool_scale': gain(ks[5], (L, MIX_WIDTH)),
        'conv_w': nrm(ks[6], (L, CONV_WIDTH, MIX_WIDTH), CONV_WIDTH ** -0.5),
        'q_norm_g': gain(ks[7], (L, ATTN_HEAD_DIM)),
        'k_norm_g': gain(ks[8], (L, ATTN_HEAD_DIM)),
        'lam_q1': nrm(ks[9], (L, ATTN_HEAD_DIM), 0.1),
        'lam_k1': nrm(ks[10], (L, ATTN_HEAD_DIM), 0.1),
        'lam_q2': nrm(ks[11], (L, ATTN_HEAD_DIM), 0.1),
        'lam_k2': nrm(ks[12], (L, ATTN_HEAD_DIM), 0.1),
        'sub_norm_g': gain(ks[13], (L, ATTN_V_DIM)),
        'w_pool_out': nrm(ks[14], (L, MIX_WIDTH, D_MODEL), MIX_WIDTH ** -0.5),
        'w_conv_out': nrm(ks[15], (L, MIX_WIDTH, D_MODEL), MIX_WIDTH ** -0.5),
        'w_attn_out': nrm(ks[16], (L, ATTN_V_WIDTH, D_MODEL), ATTN_V_WIDTH ** -0.5),
        'w_o': nrm(ks[17], (L, D_MODEL, D_MODEL), D_MODEL ** -0.5),
        'norm_mlp_g': gain(ks[18], (L, D_MODEL)),
        'w_up': nrm(ks[19], (L, D_MODEL, D_FF), D_MODEL ** -0.5),
        'w_down': nrm(ks[20], (L, D_FF, D_MODEL), D_FF ** -0.5),
        'norm_ple_g': gain(ks[21], (L, D_MODEL)),
        'w_ple_gate': nrm(ks[22], (L, D_MODEL, D_MODEL), D_MODEL ** -0.5),
        'w_ple_proj': nrm(ks[23], (L, PLE_DIM, D_MODEL), PLE_DIM ** -0.5),
    }


def reference(x, p, norm_mix_g, w_in, pool_w, pool_scale, conv_w, q_norm_g, k_norm_g,
              lam_q1, lam_k1, lam_q2, lam_k2, sub_norm_g, w_pool_out, w_conv_out,
              w_attn_out, w_o, norm_mlp_g, w_up, w_down, norm_ple_g, w_ple_gate, w_ple_proj):
    b_, s_, _ = x.shape
    sizes = [MIX_WIDTH, MIX_WIDTH, MIX_WIDTH, MIX_WIDTH,
             ATTN_QK_WIDTH, ATTN_QK_WIDTH, ATTN_V_WIDTH]
    split_at = [int(c) for c in np.cumsum(sizes)]
    h = x
    for i in range(DEPTH):
        u = rms_norm(h, norm_mix_g[i])
        proj = u @ w_in[i]
        a_in, c_x, c_b, c_c, q, k, v, gates = jnp.split(proj, split_at, axis=-1)

        y_a = multiscale_pool(a_in, pool_w[i], pool_scale[i])

        y_b = c_b * causal_depthwise_conv(c_c * c_x, conv_w[i])

        q = q.reshape(b_, s_, ATTN_HEADS, 2, ATTN_HEAD_DIM)
        k = k.reshape(b_, s_, ATTN_HEADS, 2, ATTN_HEAD_DIM)
        v = v.reshape(b_, s_, ATTN_HEADS, ATTN_V_DIM)
        q1 = rms_norm(q[..., 0, :], q_norm_g[i])
        q2 = rms_norm(q[..., 1, :], q_norm_g[i])
        k1 = rms_norm(k[..., 0, :], k_norm_g[i])
        k2 = rms_norm(k[..., 1, :], k_norm_g[i])
        lam_init = 0.8 - 0.6 * math.exp(-0.3 * i)
        lam = (jnp.exp(jnp.sum(lam_q1[i].astype(jnp.float32) * lam_k1[i].astype(jnp.float32)))
               - jnp.exp(jnp.sum(lam_q2[i].astype(jnp.float32) * lam_k2[i].astype(jnp.float32)))
               + lam_init)
        o = diff_attention(q1, q2, k1, k2, v, lam)
        o = rms_norm(o, sub_norm_g[i]) * (1.0 - lam_init)
        y_c = o.reshape(b_, s_, ATTN_V_WIDTH)

        g_a, g_b, g_c = jnp.split(jax.nn.sigmoid(gates), N_BRANCHES, axis=-1)
        merged = (g_a * (y_a @ w_pool_out[i])
                  + g_b * (y_b @ w_conv_out[i])
                  + g_c * (y_c @ w_attn_out[i]))
        h = h + merged @ w_o[i]

        m = rms_norm(h, norm_mlp_g[i])
        h = h + jnp.square(jax.nn.relu(m @ w_up[i])) @ w_down[i]

        e = rms_norm(h, norm_ple_g[i])
        h = h + jax.nn.sigmoid(e @ w_ple_gate[i]) * (p[i] @ w_ple_proj[i])
    return h
```

```python
import contextlib
import math
import numpy as np
import concourse.bass as bass
import concourse.mybir as mybir
from concourse.bass_utils import run_bass_kernel_spmd

F32 = mybir.dt.float32
AF = mybir.ActivationFunctionType
ALU = mybir.AluOpType
AX = mybir.AxisListType

P = 128
T = 512
D = 1024
KC = 8
DIN = 6656
MIX = 512
DFF = 4096
PLE = 256
HALO = 16
EPS = 1e-6
RX = 1056
C_A, C_CX, C_CB, C_CC, C_Q, C_K, C_V, C_G = 0, 512, 1024, 1536, 2048, 2560, 3072, 3584

CP_GMIX, CP_GMLP, CP_GPLE = 0, 8, 16
CP_PSC = 24
CP_CONV = 28
CP_GQ, CP_GK = 40, 41
CP_LAM = 42
CP_GSUB = 298
CP_POOLW = 426
CP_LI = 938
CP = 940
KP_PAR, KP_NPAR = 0, 1
KP_CNT = 2
KP_MC = 66
KP_ID = 72
KP = 200

EPOCH = 16000
ENGS = ("pe", "act", "dve", "pool", "sp")


class Obj:
    __slots__ = ("name", "w", "re", "rt", "const")

    def __init__(self, name, const=False):
        self.name = name
        self.w = None
        self.re = {}
        self.rt = {}
        self.const = const


class TL:
    def __init__(self, sem):
        self.sem = sem
        self.cnt = 0


class Prog:
    def __init__(self, nc, stack):
        self.nc = nc
        self.stack = stack
        self.dry = False
        self.q = {e: [] for e in ENGS}
        self.n = {e: 0 for e in ENGS}
        self.esems = {e: [] for e in ENGS}
        self.waited = {e: {} for e in ENGS}
        self.nsem = 0

    def new_sem(self, name):
        self.nsem += 1
        return self.stack.enter_context(self.nc.semaphore(name))

    def new_tl(self, name):
        return TL(self.new_sem(name))

    def _eng_sv(self, e, n):
        k = (n - 1) // EPOCH
        while len(self.esems[e]) <= k:
            self.esems[e].append(self.new_sem("e_%s_%d" % (e, len(self.esems[e]))))
        return self.esems[e][k], n - k * EPOCH

    def _resolve(self, eng, reads, writes, dma):
        need_e = {}
        need_t = {}

        def add(d):
            if d is None:
                return
            if d[0] == "eng":
                if need_e.get(d[1], 0) < d[2]:
                    need_e[d[1]] = d[2]
            else:
                if need_t.get(d[1], 0) < d[2]:
                    need_t[d[1]] = d[2]

        for o in reads:
            add(o.w)
        for o in writes:
            w = o.w
            if w is not None and not (w[0] == "eng" and w[1] == eng and not dma):
                add(w)
            for e2, n2 in o.re.items():
                if e2 == eng and not dma:
                    continue
                add(("eng", e2, n2))
            for t2, v2 in o.rt.items():
                add(("tl", t2, v2))
        waits = []
        wd = self.waited[eng]
        for e2, n2 in need_e.items():
            sem, val = self._eng_sv(e2, n2)
            key = id(sem)
            if wd.get(key, 0) < val:
                wd[key] = val
                waits.append((sem, val))
        for t2, v2 in need_t.items():
            v2 = t2.cnt
            key = id(t2.sem)
            if wd.get(key, 0) < v2:
                wd[key] = v2
                waits.append((t2.sem, v2))
        return waits

    def op(self, eng, fn, reads=(), writes=()):
        if self.dry:
            return
        waits = self._resolve(eng, reads, writes, False)
        self.n[eng] += 1
        n = self.n[eng]
        sem, _ = self._eng_sv(eng, n)
        for o in reads:
            if not o.const:
                o.re[eng] = n
        for o in writes:
            o.w = ("eng", eng, n)
            o.re = {}
            o.rt = {}
        self.q[eng].append((0, waits, fn, sem))

    def dma(self, eng, out_ap, in_ap, reads, writes, tl):
        if self.dry:
            return
        waits = self._resolve(eng, reads, writes, True)
        tl.cnt += 16
        val = tl.cnt
        for o in reads:
            if not o.const:
                o.rt[tl] = val
        for o in writes:
            o.w = ("tl", tl, val)
            o.re = {}
            o.rt = {}
        self.q[eng].append((1, waits, (out_ap, in_ap), tl.sem))

    def custom(self, eng, fn, reads, writes, tl, inc):
        if self.dry:
            return
        waits = self._resolve(eng, reads, writes, True)
        tl.cnt += inc
        val = tl.cnt
        for o in reads:
            if not o.const:
                o.rt[tl] = val
        for o in writes:
            o.w = ("tl", tl, val)
            o.re = {}
            o.rt = {}
        self.q[eng].append((2, waits, fn, (tl.sem, inc)))

    def wait_tl(self, eng, tl):
        if self.dry or tl.cnt == 0:
            return
        key = id(tl.sem)
        if self.waited[eng].get(key, 0) < tl.cnt:
            self.waited[eng][key] = tl.cnt
            self.q[eng].append((3, [(tl.sem, tl.cnt)], None, None))

    def final_wait(self, eng, tls):
        waits = []
        for tl in tls:
            if tl.cnt > 0:
                waits.append((tl.sem, tl.cnt))
        self.q[eng].append((3, waits, None, None))

    def replay(self, name, e):
        for kind, waits, a, b in self.q[name]:
            for s, v in waits:
                e.wait_ge(s, v)
            if kind == 0:
                a(e).then_inc(b, 1)
            elif kind == 1:
                e.dma_start(out=a[0], in_=a[1]).then_inc(b, 16)
            elif kind == 2:
                a(e).then_inc(b[0])


class Slot:
    __slots__ = ("t", "o")

    def __init__(self, t, o):
        self.t = t
        self.o = o


class Ring:
    def __init__(self, slots):
        self.slots = slots
        self.i = 0

    def next(self):
        s = self.slots[self.i % len(self.slots)]
        self.i += 1
        return s


class Stream:
    def __init__(self, pg, name, eng, nslots, maxpend=1):
        self.pg = pg
        self.eng = eng
        self.nslots = nslots
        self.maxpend = maxpend
        self.objs = [Obj("%s%d" % (name, i)) for i in range(nslots)]
        self.tls = [pg.new_tl("%s%d" % (name, i)) for i in range(nslots)]
        self.reqs = []
        self.groups = []
        self.idx = 0
        self.emitted = 0

    def reset(self):
        self.idx = 0
        self.emitted = 0

    def get(self, parts, group=0):
        n = self.idx
        self.idx += 1
        if self.pg.dry:
            self.reqs.append(parts)
            self.groups.append(group)
            return n % self.nslots
        lim = min(len(self.reqs), n + self.nslots - (self.maxpend - 1))
        while self.emitted < lim and self.groups[self.emitted] <= group:
            m = self.emitted
            sl = m % self.nslots
            for dstfn, src, rd in self.reqs[m]:
                self.pg.dma(self.eng, dstfn(sl), src, reads=rd, writes=[self.objs[sl]], tl=self.tls[sl])
            self.emitted += 1
        return n % self.nslots


class Ctx:
    pass


def lam_init_of(layer):
    return 0.8 - 0.6 * math.exp(-0.3 * layer)


def build_program(cfg):
    NT = cfg["NT"]
    layers = cfg["layers"]
    mode = cfg["mode"]
    L = len(layers)
    TOK = NT * T
    nc = bass.Bass("TRN2", target_bir_lowering=False)
    stack = contextlib.ExitStack()
    pg = Prog(nc, stack)
    c = Ctx()
    c.nc, c.NT, c.TOK, c.L, c.mode, c.layers = nc, NT, TOK, L, mode, layers
    c.B = cfg.get("B", 4)
    c.fake_cc = cfg.get("fake_cc", False)
    c.cc_serial = cfg.get("cc_serial", True)
    do_k = mode in ("K", "F")
    do_2 = mode in ("2", "F")

    def din(name, shape):
        return nc.dram_tensor(name, shape, F32, kind="ExternalInput").ap()

    def dout(name, shape):
        return nc.dram_tensor(name, shape, F32, kind="ExternalOutput").ap()

    def dint(name, shape):
        return nc.dram_tensor(name, shape, F32, kind="Internal").ap()

    c.hT_in = din("hT_in", [D, TOK])
    c.w_in = din("w_in", [L, D, DIN])
    c.cpack = din("cpack", [L, P, CP])
    c.kpack = din("kpack", [P, KP])
    if do_2:
        c.pT = din("pT", [L, PLE, TOK])
        c.w_pool_out = din("w_pool_out", [L, MIX, D])
        c.w_conv_out = din("w_conv_out", [L, MIX, D])
        c.w_attn_out = din("w_attn_out", [L, MIX, D])
        c.w_o = din("w_o", [L, D, D])
        c.w_up = din("w_up", [L, D, DFF])
        c.w_down = din("w_down", [L, DFF, D])
        c.w_ple_gate = din("w_ple_gate", [L, D, D])
        c.w_ple_proj = din("w_ple_proj", [L, PLE, D])
        c.hT_out = dout("hT_out", [D, TOK])
    HW = 8 * NT * HALO
    mk_loc = dout if mode == "K" else dint
    mk_all = din if mode == "2" else dint
    c.XK = [[mk_loc("xk%d_%d" % (l, i), [256, 1024]) if mode != "2" else None for i in range(NT)] for l in range(L)]
    c.XV = [[mk_loc("xv%d_%d" % (l, i), [256, 1024]) if mode != "2" else None for i in range(NT)] for l in range(L)]
    c.XH = [mk_loc("xh%d" % l, [P, HW]) if mode != "2" else None for l in range(L)]
    c.XKa = [[mk_all("xka%d_%d" % (l, i), [512, 1024]) if mode != "K" else None for i in range(NT)] for l in range(L)]
    c.XVa = [[mk_all("xva%d_%d" % (l, i), [512, 1024]) if mode != "K" else None for i in range(NT)] for l in range(L)]
    c.XHa = [mk_all("xha%d" % l, [2 * P, HW]) if mode != "K" else None for l in range(L)]
    if mode == "F":
        c.Hs = [dint("hs%d" % i, [D, TOK]) for i in range(L - 1)]

    c.o_xk = [[Obj("xk%d_%d" % (l, i)) for i in range(NT)] for l in range(L)]
    c.o_xv = [[Obj("xv%d_%d" % (l, i)) for i in range(NT)] for l in range(L)]
    c.o_xh = [Obj("xh%d" % l) for l in range(L)]
    c.o_xka = [[Obj("xka%d_%d" % (l, i)) for i in range(NT)] for l in range(L)]
    c.o_xva = [[Obj("xva%d_%d" % (l, i)) for i in range(NT)] for l in range(L)]
    c.o_xha = [Obj("xha%d" % l) for l in range(L)]
    c.o_hs = [[Obj("hs%d_%d" % (l, i)) for i in range(NT)] for l in range(L)]
    c.o_dconst = Obj("dconst", const=True)

    def sb(name, shape):
        return stack.enter_context(nc.sbuf_tensor(name, shape, F32))

    def ps(name):
        return stack.enter_context(nc.psum_tensor(name, [P, 512], F32))

    c.Hb = [sb("Hbuf%d" % k, [P, KC, T]) for k in range(2)]
    c.oHb = [[Obj("Hbuf%d_%d" % (k, i)) for i in range(KC)] for k in range(2)]
    c.H, c.oH = c.Hb[0], c.oHb[0]
    c.hsel = 0
    c.U = sb("U", [P, KC, T]); c.oU = [Obj("U%d" % i) for i in range(KC)]
    c.ones = sb("ones", [P, P]); c.blk = sb("blk", [P, P])
    c.eps = sb("eps", [P, 1])
    c.o_const = Obj("const", const=True)
    c.o_cm = Obj("constm", const=True)
    c.cp = [sb("cp%d" % l, [P, CP]) for l in range(L)]
    c.kp = sb("kp", [P, KP])
    c.lam = [sb("lam%d" % l, [P, 4]) for l in range(L)]
    c.gsubs = [sb("gsubs%d" % l, [P, P]) for l in range(L)]
    c.o_lam = [Obj("lam%d" % l) for l in range(L)]
    c.lamtmp = sb("lamtmp", [P, 64]); c.o_lamtmp = Obj("lamtmp")
    nta, ntb, ntg, ntm = 2, 2, 3, 2
    c.TA = [Slot(sb("TA%d" % i, [P, T]), Obj("TA%d" % i)) for i in range(nta)]
    c.TB = [Slot(sb("TB%d" % i, [P, T]), Obj("TB%d" % i)) for i in range(ntb)]
    c.TG = [Slot(sb("TG%d" % i, [P, T]), Obj("TG%d" % i)) for i in range(ntg)]
    c.TM = [Slot(sb("TM%d" % i, [P, T]), Obj("TM%d" % i)) for i in range(ntm)]
    NW = 8
    c.Wt = [sb("W%d" % i, [P, 1024]) for i in range(NW)]
    c.W = Stream(pg, "W", "sp", NW, maxpend=4)
    c.t_hb = [pg.new_tl("ldh0"), pg.new_tl("ldh1")]
    c.t_p = pg.new_tl("ldp")
    c.t_c = pg.new_tl("ldc")
    c.t_cc = pg.new_tl("cc")
    c.t_sk = pg.new_tl("stk"); c.t_sv = pg.new_tl("stv"); c.t_sx = pg.new_tl("stx"); c.t_so = pg.new_tl("sto")
    if do_k:
        c.UH = sb("UH", [P, KC, NT * HALO]); c.oUH = Obj("UH")
        c.HS = sb("HS", [P, 8, NT, HALO]); c.oHS = Obj("HS")
    if do_2:
        c.RY = sb("RY", [P, 16, T]); c.oRY = [Obj("RY%d" % i) for i in range(16)]
        c.M = sb("M", [P, KC, T]); c.oM = [Obj("M%d" % i) for i in range(KC)]
        c.PT = sb("PT", [P, 2, T]); c.oPT = Obj("PT")
        c.A0 = sb("A0", [P, HALO + T]); c.oA0 = Obj("A0")
        c.B1 = sb("B1", [P, HALO + T]); c.oB1 = Obj("B1")
        c.B2 = sb("B2", [P, HALO + T]); c.oB2 = Obj("B2")
        c.Z = sb("Z", [P, 2 + T]); c.oZ = Obj("Z")
        c.H0 = sb("H0", [P, 8, HALO]); c.oH0 = Obj("H0")
        c.H1 = sb("H1", [P, 8, HALO]); c.oH1 = Obj("H1")
        c.HB = sb("HB", [P, 8, HALO]); c.oHB = Obj("HB")
        NKV = 4
        c.Kt = [sb("Kt%d" % i, [P, T]) for i in range(NKV)]
        c.Vt = [sb("Vt%d" % i, [P, 4, 129]) for i in range(NKV)]
        c.KV = Stream(pg, "KV", "pool", NKV, maxpend=2)
        c.PB = [Slot(sb("PB%d" % i, [P, T]), Obj("PB%d" % i)) for i in range(4)]
        c.sm = [Slot(sb("sm%d" % i, [P, 8]), Obj("sm%d" % i)) for i in range(8)]
        c.ob = [Slot(sb("ob%d" % i, [P, P]), Obj("ob%d" % i)) for i in range(8)]
        c.jk = [Slot(sb("jk%d" % i, [P, P]), Obj("jk%d" % i)) for i in range(2)]
    c.PS = [Slot(ps("ps%d" % i), Obj("ps%d" % i)) for i in range(4)]
    c.OB = [Slot(ps("pob%d" % i), Obj("pob%d" % i)) for i in range(4)]

    for dry in (True, False):
        pg.dry = dry
        c.W.reset()
        if do_2:
            c.KV.reset()
        emit_all(c, pg)
    pg.dry = False
    tls = [c.t_sk, c.t_sv, c.t_sx, c.t_so]
    pg.final_wait("sp", tls)

    with nc.Block() as block:
        @block.tensor
        def _(e):
            pg.replay("pe", e)

        @block.scalar
        def _(e):
            pg.replay("act", e)

        @block.vector
        def _(e):
            pg.replay("dve", e)

        @block.gpsimd
        def _(e):
            pg.replay("pool", e)

        @block.sync
        def _(e):
            pg.replay("sp", e)

    stack.close()
    return nc, {e: len(pg.q[e]) for e in ENGS}


def emit_all(c, pg):
    nc = c.nc
    R = Ctx()
    R.ps = Ring(c.PS)
    R.ta, R.tb, R.tg, R.tm = Ring(c.TA), Ring(c.TB), Ring(c.TG), Ring(c.TM)
    if c.mode != "K":
        R.pb = Ring(c.PB); R.sm = Ring(c.sm); R.ob = Ring(c.ob); R.jk = Ring(c.jk)
    c.hsel = 0
    use_h(c, 0)
    emit_setup(c, pg, R)
    for li in range(c.L):
        lay = c.layers[li]
        hsrc = c.hT_in if (li == 0) else c.Hs[li - 1]
        hsrc_objs = None if li == 0 else c.o_hs[li - 1]
        if c.mode in ("K", "F"):
            emit_phase_k(c, pg, R, li, hsrc, hsrc_objs)
        if c.mode in ("2", "F"):
            last = (li == c.L - 1)
            hdst = c.hT_out if (last or c.mode == "2") else c.Hs[li]
            hdst_objs = c.o_hs[li]
            emit_phase_2(c, pg, R, li, lay, hsrc, hsrc_objs, hdst, hdst_objs)


def emit_setup(c, pg, R):
    nc = c.nc
    oc = c.o_const
    om = c.o_cm
    pg.op("dve", lambda e: e.memset(c.ones[:, :], 1.0), writes=[om])
    pg.op("dve", lambda e: e.memset(c.blk[:, :], 0.0), writes=[om])
    pg.op("dve", lambda e: e.memset(c.blk[0:64, 0:64], 1.0), writes=[om])
    pg.op("dve", lambda e: e.memset(c.blk[64:128, 64:128], 1.0), writes=[om])
    pg.op("dve", lambda e: e.memset(c.eps[:, :], EPS), writes=[om])
    for l in range(c.L):
        pg.dma("sp", c.cp[l][:, :], c.cpack[l], reads=[c.o_dconst], writes=[oc], tl=c.t_c)
    pg.dma("sp", c.kp[:, :], c.kpack, reads=[c.o_dconst], writes=[oc], tl=c.t_c)
    c.ident = c.kp[:, KP_ID:KP_ID + P]
    if c.mode != "K":
        for i in range(len(c.Vt)):
            pg.op("dve", lambda e, i=i: e.memset(c.Vt[i][:, :, 128:129], 1.0), writes=[c.KV.objs[i]])
        for l in range(c.L):
            cp = c.cp[l]
            lam = c.lam[l]
            ol = c.o_lam[l]
            for k in range(2):
                a = CP_LAM + (2 * k) * 64
                b = CP_LAM + (2 * k + 1) * 64
                pg.op("dve", lambda e, a=a, b=b, cp=cp: e.tensor_tensor(out=c.lamtmp[:, :], in0=cp[:, a:a + 64], in1=cp[:, b:b + 64], op=ALU.mult),
                      reads=[oc, c.o_lamtmp], writes=[c.o_lamtmp])
                pg.op("dve", lambda e, k=k, lam=lam: e.reduce_sum(out=lam[:, k:k + 1], in_=c.lamtmp[:, :], axis=AX.X),
                      reads=[c.o_lamtmp, ol], writes=[ol])
            pg.op("act", lambda e, lam=lam: e.activation(out=lam[:, 0:2], in_=lam[:, 0:2], func=AF.Exp), reads=[ol], writes=[ol])
            pg.op("dve", lambda e, lam=lam: e.tensor_tensor(out=lam[:, 2:3], in0=lam[:, 1:2], in1=lam[:, 0:1], op=ALU.subtract),
                  reads=[ol], writes=[ol])
            pg.op("dve", lambda e, lam=lam, cp=cp: e.tensor_tensor(out=lam[:, 3:4], in0=lam[:, 2:3], in1=cp[:, CP_LI:CP_LI + 1], op=ALU.add),
                  reads=[ol, oc], writes=[ol])
            pg.op("dve", lambda e, l=l, cp=cp: e.tensor_scalar(out=c.gsubs[l][:, :], in0=cp[:, CP_GSUB:CP_GSUB + P], scalar1=cp[:, CP_LI + 1:CP_LI + 2], scalar2=None, op0=ALU.mult),
                  reads=[oc, ol], writes=[ol])


def load_h(c, pg, hsrc, hsrc_objs, i, k):
    src = hsrc.rearrange("(kc p) t -> p kc t", p=P)[:, :, i * T:(i + 1) * T]
    rd = [c.o_dconst] if hsrc_objs is None else [hsrc_objs[i]]
    half = KC // 2
    Hk, oHk = c.Hb[k], c.oHb[k]
    pg.dma("sp", Hk[:, 0:half, :], src[:, 0:half, :], reads=rd, writes=oHk[0:half], tl=c.t_hb[k])
    pg.dma("sp", Hk[:, half:KC, :], src[:, half:KC, :], reads=rd, writes=oHk[half:KC], tl=c.t_hb[k])


def use_h(c, k):
    c.H, c.oH = c.Hb[k], c.oHb[k]


def emit_norm(c, pg, R, gcol0, cp):
    ss = R.ps.next()
    for ch in range(KC):
        sq = R.ta.next()
        pg.op("act", lambda e, sq=sq, ch=ch, H=c.H: e.activation(out=sq.t[:, :], in_=H[:, ch, :], func=AF.Square),
              reads=[c.oH[ch]], writes=[sq.o])
        pg.op("pe", lambda e, sq=sq, ch=ch, ss=ss: e.matmul(ss.t[:, :], lhsT=c.ones[:, :], rhs=sq.t[:, :], start=(ch == 0), stop=(ch == KC - 1)),
              reads=[sq.o, c.o_cm], writes=[ss.o])
    rs = R.tb.next()
    pg.op("act", lambda e, rs=rs, ss=ss: e.activation(out=rs.t[:, :], in_=ss.t[:, :], func=AF.Sqrt, bias=c.eps[:, 0:1], scale=1.0 / D),
          reads=[ss.o, c.o_cm], writes=[rs.o])
    pg.op("dve", lambda e, rs=rs: e.reciprocal(out=rs.t[:, :], in_=rs.t[:, :]), reads=[rs.o], writes=[rs.o])
    for ch in range(KC):
        pg.op("dve", lambda e, ch=ch, rs=rs, H=c.H: e.scalar_tensor_tensor(out=c.U[:, ch, :], in0=H[:, ch, :], scalar=cp[:, gcol0 + ch:gcol0 + ch + 1],
                                                                   in1=rs.t[:, :], op0=ALU.mult, op1=ALU.mult),
              reads=[c.oH[ch], rs.o, c.o_const], writes=[c.oU[ch]])


def wreq(c, src3, kcn, ncol):
    def dst(sl):
        return c.Wt[sl][:, 0:kcn * ncol].rearrange("p (k n) -> p k n", k=kcn)
    sl = c.W.get([(dst, src3, [c.o_dconst])])
    return dst(sl), c.W.objs[sl]


def proj_u(c, pg, R, w2d, col0, ncol=P, n_tok=None, rhs_fn=None, rhs_objs=None, kcn=KC):
    src = w2d.rearrange("(kc p) n -> p kc n", p=P)[:, :, col0:col0 + ncol]
    wv, wo = wreq(c, src, kcn, ncol)
    out = R.ps.next()
    if rhs_fn is None:
        rhs_fn = lambda kc: c.U[:, kc, :]
        rhs_objs = c.oU
    nt = T if n_tok is None else n_tok
    for kc in range(kcn):
        pg.op("pe", lambda e, kc=kc, out=out, wv=wv: e.matmul(out.t[:, 0:nt], lhsT=wv[:, kc, :], rhs=rhs_fn(kc), start=(kc == 0), stop=(kc == kcn - 1)),
              reads=[wo, rhs_objs[kc]], writes=[out.o])
    return out


def qk_norm(c, pg, R, xps, gcol_ap, out_ap, out_obj):
    sq = R.ta.next()
    pg.op("act", lambda e: e.activation(out=sq.t[:, :], in_=xps.t[:, :], func=AF.Square), reads=[xps.o], writes=[sq.o])
    ss = R.ps.next()
    pg.op("pe", lambda e: e.matmul(ss.t[:, :], lhsT=c.blk[:, :], rhs=sq.t[:, :], start=True, stop=True),
          reads=[sq.o, c.o_cm], writes=[ss.o])
    rs = R.tb.next()
    pg.op("act", lambda e: e.activation(out=rs.t[:, :], in_=ss.t[:, :], func=AF.Sqrt, bias=c.eps[:, 0:1], scale=1.0 / 64),
          reads=[ss.o, c.o_cm], writes=[rs.o])
    pg.op("dve", lambda e: e.reciprocal(out=rs.t[:, :], in_=rs.t[:, :]), reads=[rs.o], writes=[rs.o])
    pg.op("dve", lambda e: e.scalar_tensor_tensor(out=out_ap, in0=xps.t[:, :], scalar=gcol_ap, in1=rs.t[:, :], op0=ALU.mult, op1=ALU.mult),
          reads=[xps.o, rs.o, c.o_const], writes=[out_obj])


def v512(ap2):
    return ap2.rearrange("a (b c) -> (a b) c", b=2)


def emit_phase_k(c, pg, R, li, hsrc, hsrc_objs):
    NT = c.NT
    cp = c.cp[li]
    w_in = c.w_in[li]
    load_h(c, pg, hsrc, hsrc_objs, 0, c.hsel)
    for i in range(NT):
        if c.mode == "F" and i >= 1:
            emit_gather_tile(c, pg, li, i - 1)
        use_h(c, c.hsel)
        if i + 1 < NT:
            load_h(c, pg, hsrc, hsrc_objs, i + 1, 1 - c.hsel)
        c.hsel = 1 - c.hsel
        emit_norm(c, pg, R, CP_GMIX, cp)
        pg.op("pool", lambda e, i=i: e.tensor_copy(out=c.UH[:, :, i * HALO:(i + 1) * HALO], in_=c.U[:, :, T - HALO:T]),
              reads=c.oU, writes=[c.oUH])
        for j in range(4):
            kps = proj_u(c, pg, R, w_in, C_K + j * P)
            kn = R.tg.next()
            qk_norm(c, pg, R, kps, cp[:, CP_GK:CP_GK + 1], kn.t[:, :], kn.o)
            pg.dma("sp", v512(c.XK[li][i])[j * P:(j + 1) * P, :], kn.t[:, :], reads=[kn.o], writes=[c.o_xk[li][i]], tl=c.t_sk)
        wsl = []
        for k2 in range(4):
            src = w_in.rearrange("(kc p) n -> p kc n", p=P)[:, 2 * k2:2 * k2 + 2, C_V:C_V + 512]
            wsl.append(wreq(c, src, 2, 512))
        for tb in range(4):
            vps = R.ps.next()
            for kc in range(KC):
                wv, wo = wsl[kc // 2]
                pg.op("pe", lambda e, kc=kc, tb=tb, vps=vps, wv=wv: e.matmul(vps.t[:, :], lhsT=c.U[:, kc, tb * P:(tb + 1) * P], rhs=wv[:, kc % 2, :],
                                                                              start=(kc == 0), stop=(kc == KC - 1)),
                      reads=[wo, c.oU[kc]], writes=[vps.o])
            vs = R.tg.next()
            pg.op("act", lambda e, vs=vs, vps=vps: e.activation(out=vs.t[:, :], in_=vps.t[:, :], func=AF.Copy), reads=[vps.o], writes=[vs.o])
            pg.dma("sp", v512(c.XV[li][i])[tb * P:(tb + 1) * P, :], vs.t[:, :], reads=[vs.o], writes=[c.o_xv[li][i]], tl=c.t_sv)
    NH = NT * HALO
    uh_fn = lambda kc: c.UH[:, kc, :]
    uh_objs = [c.oUH] * KC
    hs2 = c.HS[:, :, :, :].rearrange("p a b t -> p a (b t)")
    for j in range(4):
        aps = proj_u(c, pg, R, w_in, C_A + j * P, n_tok=NH, rhs_fn=uh_fn, rhs_objs=uh_objs)
        pg.op("act", lambda e, j=j, aps=aps: e.activation(out=hs2[:, j, :], in_=aps.t[:, 0:NH], func=AF.Copy), reads=[aps.o], writes=[c.oHS])
    for j in range(4):
        xps = proj_u(c, pg, R, w_in, C_CX + j * P, n_tok=NH, rhs_fn=uh_fn, rhs_objs=uh_objs)
        xs = R.tm.next()
        pg.op("act", lambda e, xs=xs, xps=xps: e.activation(out=xs.t[:, 0:NH], in_=xps.t[:, 0:NH], func=AF.Copy), reads=[xps.o], writes=[xs.o])
        cps = proj_u(c, pg, R, w_in, C_CC + j * P, n_tok=NH, rhs_fn=uh_fn, rhs_objs=uh_objs)
        pg.op("dve", lambda e, j=j, xs=xs, cps=cps: e.tensor_tensor(out=hs2[:, 4 + j, :], in0=cps.t[:, 0:NH], in1=xs.t[:, 0:NH], op=ALU.mult),
              reads=[cps.o, xs.o], writes=[c.oHS])
    pg.dma("sp", c.XH[li], c.HS[:, :, :, :].rearrange("p a b t -> p (a b t)"), reads=[c.oHS], writes=[c.o_xh[li]], tl=c.t_sx)
    if c.mode == "F":
        emit_gather_tile(c, pg, li, NT - 1)
        groups = [[2 * b, 2 * b + 1] for b in range(c.B)]
        xh, xha = c.XH[li], c.XHa[li]
        if c.fake_cc:
            for r in range(2):
                pg.dma("pool", xha[r * P:(r + 1) * P, :], xh, reads=[c.o_xh[li]], writes=[c.o_xha[li]], tl=c.t_cc)
        else:
          pg.custom("pool", lambda e: e.collective_compute("AllGather", ALU.bypass, replica_groups=groups, ins=[xh.opt()], outs=[xha.opt()]),
                  reads=[c.o_xh[li]], writes=[c.o_xha[li]], tl=c.t_cc, inc=1)
          if c.cc_serial:
            pg.wait_tl("pool", c.t_cc)


def emit_gather_tile(c, pg, li, i):
    groups = [[2 * b, 2 * b + 1] for b in range(c.B)]
    if c.fake_cc:
        for src, dst, osrc, odst in ((c.XK[li][i], c.XKa[li][i], c.o_xk[li][i], c.o_xka[li][i]),
                                     (c.XV[li][i], c.XVa[li][i], c.o_xv[li][i], c.o_xva[li][i])):
            for r in range(2):
                pg.dma("pool", dst[r * 256:(r + 1) * 256, :], src, reads=[osrc], writes=[odst], tl=c.t_cc)
        return
    for src, dst, osrc, odst in ((c.XK[li][i], c.XKa[li][i], c.o_xk[li][i], c.o_xka[li][i]),
                                 (c.XV[li][i], c.XVa[li][i], c.o_xv[li][i], c.o_xva[li][i])):
        pg.custom("pool", lambda e, src=src, dst=dst: e.collective_compute("AllGather", ALU.bypass, replica_groups=groups,
                                                                         ins=[src.opt()], outs=[dst.opt()]),
                  reads=[osrc], writes=[odst], tl=c.t_cc, inc=1)
        if c.cc_serial:
            pg.wait_tl("pool", c.t_cc)


def emit_phase_2(c, pg, R, li, lay, hsrc, hsrc_objs, hdst, hdst_objs):
    NT = c.NT
    cp = c.cp[li]
    kp = c.kp
    w_in = c.w_in[li]
    fz = (c.mode == "F")
    HW = 8 * NT * HALO
    oxha = c.o_xha[li] if fz else c.o_dconst
    hviews = [c.XHa[li][r * P:(r + 1) * P, :].rearrange("p (a b t) -> p a b t", a=8, b=NT) for r in range(2)]
    lam = c.lam[li]
    ol = c.o_lam[li]
    YA = lambda j: c.RY[:, j, :]
    YB = lambda j: c.RY[:, 4 + j, :]
    YC = lambda j: c.RY[:, 8 + j, :]
    QN = lambda j: c.RY[:, 12 + j, :]
    oYA, oYB, oYC, oQN = c.oRY[0:4], c.oRY[4:8], c.oRY[8:12], c.oRY[12:16]
    oc = c.o_const
    load_h(c, pg, hsrc, hsrc_objs, 0, c.hsel)
    for i in range(NT):
        use_h(c, c.hsel)
        c.hsel = 1 - c.hsel
        pg.dma("sp", c.PT[:, :, :], c.pT[li].rearrange("(k p) t -> p k t", p=P)[:, :, i * T:(i + 1) * T],
               reads=[c.o_dconst], writes=[c.oPT], tl=c.t_p)
        h0v = hviews[0][:, :, i, :]
        pg.dma("sp", c.H0[:, :, :], h0v, reads=[oxha], writes=[c.oH0], tl=c.t_p)
        if i > 0:
            h1v = hviews[1][:, :, i - 1, :]
            pg.dma("sp", c.H1[:, :, :], h1v, reads=[oxha], writes=[c.oH1], tl=c.t_p)
        else:
            pg.op("dve", lambda e: e.memset(c.H1[:, :, :], 0.0), writes=[c.oH1])
        pg.op("dve", lambda e: e.tensor_scalar(out=c.HB[:, :, :], in0=c.H0[:, :, :], scalar1=kp[:, KP_PAR:KP_PAR + 1], scalar2=None, op0=ALU.mult),
              reads=[c.oH0, oc], writes=[c.oHB])
        pg.op("dve", lambda e: e.scalar_tensor_tensor(out=c.HB[:, :, :], in0=c.H1[:, :, :], scalar=kp[:, KP_NPAR:KP_NPAR + 1], in1=c.HB[:, :, :],
                                                      op0=ALU.mult, op1=ALU.add),
              reads=[c.oH1, c.oHB, oc], writes=[c.oHB])
        emit_norm(c, pg, R, CP_GMIX, cp)

        for g in range(4):
            aps = proj_u(c, pg, R, w_in, C_A + g * P)
            pg.op("dve", lambda e, g=g: e.tensor_copy(out=c.A0[:, 0:HALO], in_=c.HB[:, g, :]), reads=[c.oHB], writes=[c.oA0])
            pg.op("act", lambda e, aps=aps: e.activation(out=c.A0[:, HALO:HALO + T], in_=aps.t[:, :], func=AF.Copy), reads=[aps.o], writes=[c.oA0])
            cur, ocur = c.A0, c.oA0
            bufs = [(c.B1, c.oB1), (c.B2, c.oB2)]
            sh = 1
            lo = 0
            for st in range(g + 1):
                nb, onb = bufs[st % 2]
                lo = lo + sh
                W_ = HALO + T
                pg.op("dve", lambda e, cur=cur, nb=nb, lo=lo, sh=sh, W_=W_: e.tensor_tensor(out=nb[:, lo:W_], in0=cur[:, lo:W_], in1=cur[:, lo - sh:W_ - sh], op=ALU.add),
                      reads=[ocur], writes=[onb])
                cur, ocur = nb, onb
                sh *= 2
            win = float(2 ** (g + 1))
            pl = R.tm.next()
            pg.op("dve", lambda e, cur=cur, pl=pl, win=win: e.scalar_tensor_tensor(out=pl.t[:, :], in0=cur[:, HALO:HALO + T], scalar=1.0 / win, in1=c.A0[:, HALO:HALO + T],
                                                                                 op0=ALU.mult, op1=ALU.subtract),
                  reads=[ocur, c.oA0], writes=[pl.o])
            if i == 0:
                pg.op("dve", lambda e, cur=cur, pl=pl, g=g: e.tensor_tensor(out=pl.t[:, 0:HALO], in0=cur[:, HALO:2 * HALO],
                                                                           in1=kp[:, KP_CNT + g * HALO:KP_CNT + (g + 1) * HALO], op=ALU.mult),
                      reads=[ocur, oc, pl.o], writes=[pl.o])
                pg.op("dve", lambda e, pl=pl: e.tensor_tensor(out=pl.t[:, 0:HALO], in0=pl.t[:, 0:HALO], in1=c.A0[:, HALO:2 * HALO], op=ALU.subtract),
                      reads=[pl.o, c.oA0], writes=[pl.o])
            mps = R.ps.next()
            pg.op("pe", lambda e, g=g, pl=pl, mps=mps: e.matmul(mps.t[:, :], lhsT=cp[:, CP_POOLW + g * P:CP_POOLW + (g + 1) * P], rhs=pl.t[:, :], start=True, stop=True),
                  reads=[pl.o, oc], writes=[mps.o])
            pg.op("act", lambda e, g=g, mps=mps: e.activation(out=YA(g), in_=mps.t[:, :], func=AF.Copy, scale=cp[:, CP_PSC + g:CP_PSC + g + 1]),
                  reads=[mps.o, oc], writes=[oYA[g]])

        for j in range(4):
            xps = proj_u(c, pg, R, w_in, C_CX + j * P)
            xs = R.tg.next()
            pg.op("act", lambda e, xs=xs, xps=xps: e.activation(out=xs.t[:, :], in_=xps.t[:, :], func=AF.Copy), reads=[xps.o], writes=[xs.o])
            cps = proj_u(c, pg, R, w_in, C_CC + j * P)
            pg.op("dve", lambda e, j=j: e.tensor_copy(out=c.Z[:, 0:2], in_=c.HB[:, 4 + j, HALO - 2:HALO]), reads=[c.oHB], writes=[c.oZ])
            pg.op("dve", lambda e, xs=xs, cps=cps: e.tensor_tensor(out=c.Z[:, 2:2 + T], in0=cps.t[:, :], in1=xs.t[:, :], op=ALU.mult),
                  reads=[cps.o, xs.o], writes=[c.oZ])
            bps = proj_u(c, pg, R, w_in, C_CB + j * P)
            tc = R.tm.next()
            cw = lambda k, j=j: cp[:, CP_CONV + k * 4 + j:CP_CONV + k * 4 + j + 1]
            pg.op("dve", lambda e, tc=tc, cw=cw: e.tensor_scalar(out=tc.t[:, :], in0=c.Z[:, 0:T], scalar1=cw(0), scalar2=None, op0=ALU.mult),
                  reads=[c.oZ, oc], writes=[tc.o])
            pg.op("dve", lambda e, tc=tc, cw=cw: e.scalar_tensor_tensor(out=tc.t[:, :], in0=c.Z[:, 1:1 + T], scalar=cw(1), in1=tc.t[:, :], op0=ALU.mult, op1=ALU.add),
                  reads=[c.oZ, tc.o, oc], writes=[tc.o])
            pg.op("dve", lambda e, tc=tc, cw=cw: e.scalar_tensor_tensor(out=tc.t[:, :], in0=c.Z[:, 2:2 + T], scalar=cw(2), in1=tc.t[:, :], op0=ALU.mult, op1=ALU.add),
                  reads=[c.oZ, tc.o, oc], writes=[tc.o])
            pg.op("dve", lambda e, j=j, tc=tc, bps=bps: e.tensor_tensor(out=YB(j), in0=bps.t[:, :], in1=tc.t[:, :], op=ALU.mult),
                  reads=[bps.o, tc.o], writes=[oYB[j]])

        for h in range(4):
            qps = proj_u(c, pg, R, w_in, C_Q + h * P)
            qk_norm(c, pg, R, qps, cp[:, CP_GQ:CP_GQ + 1], QN(h), oQN[h])
        nkt = 2 * i + 2
        seq = [(h, gk, kb) for h in range(4) for gk in range(nkt) for kb in range(4)]
        kvcur = {}

        def att_S(h, gk, kb):
            if kb == 0:
                rank, lt = gk % 2, gk // 2
                ksrc = v512(c.XKa[li][lt][rank * 256:(rank + 1) * 256, :])[h * P:(h + 1) * P, :]
                vsrc = v512(c.XVa[li][lt][rank * 256:(rank + 1) * 256, :])[:, h * P:(h + 1) * P].rearrange("(kb p) d -> p kb d", p=P)
                oka = c.o_xka[li][lt] if fz else c.o_dconst
                ova = c.o_xva[li][lt] if fz else c.o_dconst
                sl = c.KV.get([(lambda s: c.Kt[s][:, :], ksrc, [oka]), (lambda s: c.Vt[s][:, :, 0:P], vsrc, [ova])], group=li)
                kvcur[(h, gk)] = sl
            sl = kvcur[(h, gk)]
            okv = c.KV.objs[sl]
            Kt, Vt = c.Kt[sl], c.Vt[sl]
            slot_m = gk - 2 * i
            pbs = []
            for w in range(2):
                sps = R.ps.next()
                p0 = 64 * w
                pg.op("pe", lambda e, sps=sps, p0=p0, kb=kb, Kt=Kt, h=h: e.matmul(sps.t[:, :], lhsT=Kt[p0:p0 + 64, kb * P:(kb + 1) * P],
                                                                               rhs=c.RY[p0:p0 + 64, 12 + h, :], start=True, stop=True),
                      reads=[okv, oQN[h]], writes=[sps.o])
                pb = R.pb.next()
                pg.op("act", lambda e, pb=pb, sps=sps: e.activation(out=pb.t[:, :], in_=sps.t[:, :], func=AF.Exp, scale=0.125),
                      reads=[sps.o], writes=[pb.o])
                if slot_m >= 0:
                    ranges = [(0, kb * P), (kb * P, kb * P + 64), (kb * P + 64, T)]
                    for ri, (a_, b_) in enumerate(ranges):
                        if b_ <= a_:
                            continue
                        col = KP_MC + slot_m * 3 + ri
                        pg.op("dve", lambda e, pb=pb, a_=a_, b_=b_, col=col: e.tensor_scalar(out=pb.t[:, a_:b_], in0=pb.t[:, a_:b_], scalar1=kp[:, col:col + 1],
                                                                                         scalar2=None, op0=ALU.mult),
                              reads=[pb.o, oc], writes=[pb.o])
                pbs.append(pb)
            return pbs, Vt, okv

        def att_PV(h, gk, kb, info):
            pbs, Vt, okv = info
            first = (gk == 0 and kb == 0)
            last = (gk == nkt - 1 and kb == 3)
            slot_m = gk - 2 * i
            for qb in range(4):
                if slot_m == 1 and qb < kb:
                    continue
                for w in range(2):
                    a_i = qb * 2 + w
                    bank, pos = a_i // 3, a_i % 3
                    ob = c.OB[bank]
                    st = first and pos == 0
                    pg.op("pe", lambda e, ob=ob, pos=pos, pb=pbs[w], qb=qb, Vt=Vt, kb=kb, st=st, last=last:
                          e.matmul(ob.t[:, pos * 129:(pos + 1) * 129], lhsT=pb.t[:, qb * P:(qb + 1) * P], rhs=Vt[:, kb, :],
                                   start=st, stop=last, skip_group_check=True),
                          reads=[pbs[w].o, okv], writes=[ob.o])

        def att_fin_pre(h):
            outs = []
            for qb in range(4):
                acc = []
                for w in range(2):
                    a_i = qb * 2 + w
                    bank, pos = a_i // 3, a_i % 3
                    acc.append((c.OB[bank], pos * 129))
                sm = R.sm.next()
                o1 = R.ob.next()
                (b1, c1), (b2, c2) = acc
                pg.op("dve", lambda e, sm=sm, b1=b1, c1=c1: e.reciprocal(out=sm.t[:, 0:1], in_=b1.t[:, c1 + 128:c1 + 129]), reads=[b1.o], writes=[sm.o])
                pg.op("dve", lambda e, sm=sm, b2=b2, c2=c2: e.reciprocal(out=sm.t[:, 1:2], in_=b2.t[:, c2 + 128:c2 + 129]), reads=[b2.o, sm.o], writes=[sm.o])
                pg.op("dve", lambda e, sm=sm: e.tensor_tensor(out=sm.t[:, 2:3], in0=sm.t[:, 1:2], in1=lam[:, 3:4], op=ALU.mult), reads=[sm.o, ol], writes=[sm.o])
                pg.op("dve", lambda e, sm=sm, o1=o1, b1=b1, c1=c1: e.tensor_scalar(out=o1.t[:, :], in0=b1.t[:, c1:c1 + 128], scalar1=sm.t[:, 0:1], scalar2=None, op0=ALU.mult),
                      reads=[b1.o, sm.o], writes=[o1.o])
                pg.op("dve", lambda e, sm=sm, o1=o1, b2=b2, c2=c2: e.scalar_tensor_tensor(out=o1.t[:, :], in0=b2.t[:, c2:c2 + 128], scalar=sm.t[:, 2:3], in1=o1.t[:, :],
                                                                                           op0=ALU.mult, op1=ALU.add),
                      reads=[b2.o, sm.o, o1.o], writes=[o1.o])
                jk = R.jk.next()
                pg.op("act", lambda e, o1=o1, jk=jk: e.activation(out=jk.t[:, :], in_=o1.t[:, :], func=AF.Square),
                      reads=[o1.o], writes=[jk.o])
                pg.op("dve", lambda e, sm=sm, jk=jk: e.reduce_sum(out=sm.t[:, 3:4], in_=jk.t[:, :], axis=AX.X), reads=[jk.o, sm.o], writes=[sm.o])
                pg.op("act", lambda e, sm=sm: e.activation(out=sm.t[:, 4:5], in_=sm.t[:, 3:4], func=AF.Sqrt, bias=c.eps[:, 0:1], scale=1.0 / 128),
                      reads=[sm.o, c.o_cm], writes=[sm.o])
                pg.op("dve", lambda e, sm=sm: e.reciprocal(out=sm.t[:, 5:6], in_=sm.t[:, 4:5]), reads=[sm.o], writes=[sm.o])
                pg.op("dve", lambda e, sm=sm, o1=o1: e.scalar_tensor_tensor(out=o1.t[:, :], in0=o1.t[:, :], scalar=sm.t[:, 5:6], in1=c.gsubs[li][:, :], op0=ALU.mult, op1=ALU.mult),
                      reads=[o1.o, sm.o, ol], writes=[o1.o])
                outs.append((o1, qb))
            return h, outs

        def att_fin_T(fin):
            h, outs = fin
            for o1, qb in outs:
                tps = R.ps.next()
                pg.op("pe", lambda e, tps=tps, o1=o1: e.transpose(tps.t[:, 0:P], o1.t[:, :], c.ident[:, :]), reads=[o1.o, oc], writes=[tps.o])
                pg.op("act", lambda e, tps=tps, h=h, qb=qb: e.activation(out=c.RY[:, 8 + h, qb * P:(qb + 1) * P], in_=tps.t[:, 0:P], func=AF.Copy),
                      reads=[tps.o], writes=[oYC[h]])

        infos = {0: att_S(*seq[0])}
        pending_T = None
        since = 0
        for idx in range(len(seq)):
            if idx + 1 < len(seq):
                infos[idx + 1] = att_S(*seq[idx + 1])
            h, gk, kb = seq[idx]
            att_PV(h, gk, kb, infos.pop(idx))
            since += 1
            if pending_T is not None and since == 3:
                att_fin_T(pending_T)
                pending_T = None
            if gk == nkt - 1 and kb == 3:
                assert pending_T is None
                pending_T = att_fin_pre(h)
                since = 0

        if i + 1 < NT:
            load_h(c, pg, hsrc, hsrc_objs, i + 1, c.hsel)
        wbr = [(c.w_pool_out[li], YA, oYA), (c.w_conv_out[li], YB, oYB), (c.w_attn_out[li], YC, oYC)]
        for j in range(KC):
            gs = []
            for b in range(3):
                gps = proj_u(c, pg, R, w_in, C_G + b * D + j * P)
                g = R.tg.next()
                pg.op("act", lambda e, g=g, gps=gps: e.activation(out=g.t[:, :], in_=gps.t[:, :], func=AF.Sigmoid), reads=[gps.o], writes=[g.o])
                gs.append(g)
            if pending_T is not None:
                att_fin_T(pending_T)
                pending_T = None
            for b in range(3):
                w2d, yf, yo = wbr[b]
                pps = proj_u(c, pg, R, w2d, j * P, rhs_fn=yf, rhs_objs=yo, kcn=4)
                if b == 0:
                    pg.op("dve", lambda e, j=j, g=gs[b], pps=pps: e.tensor_tensor(out=c.M[:, j, :], in0=pps.t[:, :], in1=g.t[:, :], op=ALU.mult),
                          reads=[pps.o, gs[b].o], writes=[c.oM[j]])
                else:
                    pg.op("dve", lambda e, g=gs[b], pps=pps: e.tensor_tensor(out=g.t[:, :], in0=pps.t[:, :], in1=g.t[:, :], op=ALU.mult),
                          reads=[pps.o, gs[b].o], writes=[gs[b].o])
                    pg.op("pool", lambda e, j=j, g=gs[b]: e.tensor_tensor(out=c.M[:, j, :], in0=c.M[:, j, :], in1=g.t[:, :], op=ALU.add),
                          reads=[c.oM[j], gs[b].o], writes=[c.oM[j]])
        mfn = lambda kc: c.M[:, kc, :]
        for j in range(KC):
            ops_ = proj_u(c, pg, R, c.w_o[li], j * P, rhs_fn=mfn, rhs_objs=c.oM)
            pg.op("dve", lambda e, j=j, ops_=ops_, H=c.H: e.tensor_tensor(out=H[:, j, :], in0=ops_.t[:, :], in1=H[:, j, :], op=ALU.add),
                  reads=[ops_.o, c.oH[j]], writes=[c.oH[j]])

        emit_norm(c, pg, R, CP_GMLP, cp)
        rfn = lambda f: c.RY[:, f, :]
        for fh in range(2):
            for f in range(16):
                ups = proj_u(c, pg, R, c.w_up[li], (fh * 16 + f) * P)
                pg.op("act", lambda e, f=f, ups=ups: e.activation(out=c.RY[:, f, :], in_=ups.t[:, :], func=AF.Relu), reads=[ups.o], writes=[c.oRY[f]])
                pg.op("pool", lambda e, f=f: e.tensor_tensor(out=c.RY[:, f, :], in0=c.RY[:, f, :], in1=c.RY[:, f, :], op=ALU.mult),
                      reads=[c.oRY[f]], writes=[c.oRY[f]])
            for chf in range(2):
                for f2 in range(8):
                    r0 = (fh * 16 + 2 * f2) * P
                    src = c.w_down[li][r0:r0 + 2 * P, chf * 512:(chf + 1) * 512].rearrange("(k p) n -> p k n", p=P)
                    wv, wo = wreq(c, src, 2, 512)
                    for k in range(2):
                        f = 2 * f2 + k
                        for j in range(4):
                            ob = c.OB[j]
                            pg.op("pe", lambda e, ob=ob, wv=wv, k=k, j=j, f=f: e.matmul(ob.t[:, :], lhsT=wv[:, k, j * P:(j + 1) * P], rhs=c.RY[:, f, :],
                                                                                       start=(f == 0), stop=(f == 15)),
                                  reads=[wo, c.oRY[f]], writes=[ob.o])
                for j in range(4):
                    jj = chf * 4 + j
                    ob = c.OB[j]
                    pg.op("dve", lambda e, jj=jj, ob=ob, H=c.H: e.tensor_tensor(out=H[:, jj, :], in0=ob.t[:, :], in1=H[:, jj, :], op=ALU.add),
                          reads=[ob.o, c.oH[jj]], writes=[c.oH[jj]])

        emit_norm(c, pg, R, CP_GPLE, cp)
        pfn = lambda k: c.PT[:, k, :]
        for j in range(KC):
            gps = proj_u(c, pg, R, c.w_ple_gate[li], j * P)
            g = R.tg.next()
            pg.op("act", lambda e, g=g, gps=gps: e.activation(out=g.t[:, :], in_=gps.t[:, :], func=AF.Sigmoid), reads=[gps.o], writes=[g.o])
            pps = proj_u(c, pg, R, c.w_ple_proj[li], j * P, rhs_fn=pfn, rhs_objs=[c.oPT, c.oPT], kcn=2)
            pg.op("dve", lambda e, g=g, pps=pps: e.tensor_tensor(out=g.t[:, :], in0=pps.t[:, :], in1=g.t[:, :], op=ALU.mult),
                  reads=[pps.o, g.o], writes=[g.o])
            pg.op("pool", lambda e, j=j, g=g, H=c.H: e.tensor_tensor(out=H[:, j, :], in0=H[:, j, :], in1=g.t[:, :], op=ALU.add),
                  reads=[c.oH[j], g.o], writes=[c.oH[j]])
        dst = hdst.rearrange("(kc p) t -> p kc t", p=P)[:, :, i * T:(i + 1) * T]
        half = KC // 2
        pg.dma("sp", dst[:, 0:half, :], c.H[:, 0:half, :], reads=c.oH[0:half], writes=[hdst_objs[i]], tl=c.t_so)
        pg.dma("sp", dst[:, half:KC, :], c.H[:, half:KC, :], reads=c.oH[half:KC], writes=[hdst_objs[i]], tl=c.t_so)


def tok_index(NT, r):
    idx = []
    for i in range(NT):
        g = 2 * i + r
        idx.append(np.arange(g * T, (g + 1) * T))
    return np.concatenate(idx)


def make_cpack(inp, l):
    cpk = np.zeros((P, CP), np.float32)
    f32 = lambda a: np.asarray(a, np.float32)
    cpk[:, CP_GMIX:CP_GMIX + 8] = f32(inp["norm_mix_g"][l]).reshape(8, P).T
    cpk[:, CP_GMLP:CP_GMLP + 8] = f32(inp["norm_mlp_g"][l]).reshape(8, P).T
    cpk[:, CP_GPLE:CP_GPLE + 8] = f32(inp["norm_ple_g"][l]).reshape(8, P).T
    cpk[:, CP_PSC:CP_PSC + 4] = f32(inp["pool_scale"][l]).reshape(4, P).T
    cw = f32(inp["conv_w"][l]).reshape(3, 4, P)
    cpk[:, CP_CONV:CP_CONV + 12] = cw.transpose(2, 0, 1).reshape(P, 12)
    cpk[:, CP_GQ] = np.tile(f32(inp["q_norm_g"][l]), 2)
    cpk[:, CP_GK] = np.tile(f32(inp["k_norm_g"][l]), 2)
    for k, nm in enumerate(("lam_q1", "lam_k1", "lam_q2", "lam_k2")):
        cpk[:, CP_LAM + k * 64:CP_LAM + (k + 1) * 64] = np.broadcast_to(f32(inp[nm][l])[None, :], (P, 64))
    cpk[:, CP_GSUB:CP_GSUB + P] = np.broadcast_to(f32(inp["sub_norm_g"][l])[None, :], (P, P))
    cpk[:, CP_POOLW:CP_POOLW + 4 * P] = f32(inp["pool_w"][l]).transpose(1, 0, 2).reshape(P, 4 * P)
    li0 = lam_init_of(l)
    cpk[:, CP_LI] = -li0
    cpk[:, CP_LI + 1] = 1.0 - li0
    return cpk


def make_kpack(r):
    kpk = np.zeros((P, KP), np.float32)
    kpk[:, KP_PAR] = float(r)
    kpk[:, KP_NPAR] = float(1 - r)
    for g, win in enumerate((2, 4, 8, 16)):
        for t in range(HALO):
            kpk[:, KP_CNT + g * HALO + t] = (1.0 / min(t + 1, win)) if r == 0 else (1.0 / win)
    low = (np.arange(P) < 64).astype(np.float32)
    diag = [np.zeros(P, np.float32), low, np.ones(P, np.float32)]
    zero = [np.zeros(P, np.float32)] * 3
    one = [np.ones(P, np.float32)] * 3
    tabs = (diag, zero) if r == 0 else (one, diag)
    for s in range(2):
        for ri in range(3):
            kpk[:, KP_MC + s * 3 + ri] = tabs[s][ri]
    kpk[:, KP_ID:KP_ID + P] = np.eye(P, dtype=np.float32)
    return kpk


_PROG_CACHE = {}


def get_prog(NT, layers_n, mode):
    key = (NT, layers_n, mode)
    if key not in _PROG_CACHE:
        _PROG_CACHE[key] = build_program({"NT": NT, "layers": list(range(layers_n)) if mode == "F" else [0], "mode": mode})[0]
    return _PROG_CACHE[key]


WNAMES2 = ("w_pool_out", "w_conv_out", "w_attn_out", "w_o", "w_up", "w_down", "w_ple_gate", "w_ple_proj")


def run_unfused(inp, NT, B, depth):
    ncores = 2 * B
    x = np.asarray(inp["x"], np.float32)
    p = np.asarray(inp["p"], np.float32)
    idxs = [tok_index(NT, r) for r in range(2)]
    hT = [np.ascontiguousarray(x[cid // 2][idxs[cid % 2]].T) for cid in range(ncores)]
    kpk = [make_kpack(cid % 2) for cid in range(ncores)]
    for l in range(depth):
        cpk = make_cpack(inp, l)[None]
        w_in = np.ascontiguousarray(np.asarray(inp["w_in"], np.float32)[l][None])
        ncK = get_prog_l(NT, 0, "K")
        maps = [{"hT_in": hT[cid], "w_in": w_in, "cpack": cpk, "kpack": kpk[cid]} for cid in range(ncores)]
        res = run_bass_kernel_spmd(ncK, maps, core_ids=list(range(ncores)))
        xnames = ["xk0_%d" % i for i in range(NT)] + ["xv0_%d" % i for i in range(NT)] + ["xh0"]
        xbs = [{nm: np.asarray(r[nm]) for nm in xnames} for r in res.results]
        nc2 = get_prog_l(NT, 0, "2")
        maps = []
        for cid in range(ncores):
            b = cid // 2
            m = {"hT_in": hT[cid], "w_in": w_in, "cpack": cpk, "kpack": kpk[cid],
                 "pT": np.ascontiguousarray(p[l, b][idxs[cid % 2]].T)[None]}
            for nm in xnames:
                m[nm.replace("xk", "xka").replace("xv", "xva").replace("xh", "xha")] = np.ascontiguousarray(
                    np.concatenate([xbs[2 * b][nm], xbs[2 * b + 1][nm]], axis=0))
            for nm in WNAMES2:
                m[nm] = np.ascontiguousarray(np.asarray(inp[nm], np.float32)[l][None])
            maps.append(m)
        res = run_bass_kernel_spmd(nc2, maps, core_ids=list(range(ncores)))
        hT = [np.asarray(r["hT_out"]) for r in res.results]
    out = np.zeros((B, 2 * NT * T, D), np.float32)
    for cid in range(ncores):
        out[cid // 2][idxs[cid % 2]] = hT[cid].T
    return out


def get_prog_l(NT, layer, mode):
    key = (NT, layer, mode)
    if key not in _PROG_CACHE:
        _PROG_CACHE[key] = build_program({"NT": NT, "layers": [layer], "mode": mode})[0]
    return _PROG_CACHE[key]


def run_fused(inp, NT, B, depth):
    ncores = 2 * B
    x = np.asarray(inp["x"], np.float32)
    p = np.asarray(inp["p"], np.float32)
    idxs = [tok_index(NT, r) for r in range(2)]
    key = (NT, depth, "F")
    if key not in _PROG_CACHE:
        _PROG_CACHE[key] = build_program({"NT": NT, "layers": list(range(depth)), "mode": "F", "B": B, "fake_cc": FAKE_CC})[0]
    nc = _PROG_CACHE[key]
    cpk = np.stack([make_cpack(inp, l) for l in range(depth)])
    shared = {"w_in": np.ascontiguousarray(np.asarray(inp["w_in"], np.float32)[:depth]), "cpack": cpk}
    for nm in WNAMES2:
        shared[nm] = np.ascontiguousarray(np.asarray(inp[nm], np.float32)[:depth])
    maps = []
    for cid in range(ncores):
        b, r = cid // 2, cid % 2
        m = dict(shared)
        m["hT_in"] = np.ascontiguousarray(x[b][idxs[r]].T)
        m["pT"] = np.ascontiguousarray(p[:depth, b][:, idxs[r]].transpose(0, 2, 1))
        m["kpack"] = make_kpack(r)
        maps.append(m)
    res = run_bass_kernel_spmd(nc, maps, core_ids=list(range(ncores)))
    out = np.zeros((B, 2 * NT * T, D), np.float32)
    for cid in range(ncores):
        out[cid // 2][idxs[cid % 2]] = np.asarray(res.results[cid]["hT_out"]).T
    return out


FUSED = True
FAKE_CC = False


def kernel(**inputs):
    if FUSED:
        return run_fused(inputs, 8, 4, 2)
    return run_unfused(inputs, 8, 4, 2)
```

```python
import contextlib
import math
import numpy as np
import concourse.bass as bass
import concourse.mybir as mybir
from concourse.bass_utils import run_bass_kernel_spmd

F32 = mybir.dt.float32
AF = mybir.ActivationFunctionType
ALU = mybir.AluOpType
AX = mybir.AxisListType

P = 128
T = 512
D = 1024
KC = 8
DIN = 6656
MIX = 512
DFF = 4096
PLE = 256
HALO = 16
EPS = 1e-6
RX = 1056
C_A, C_CX, C_CB, C_CC, C_Q, C_K, C_V, C_G = 0, 512, 1024, 1536, 2048, 2560, 3072, 3584

CP_GMIX, CP_GMLP, CP_GPLE = 0, 8, 16
CP_PSC = 24
CP_CONV = 28
CP_GQ, CP_GK = 40, 41
CP_LAM = 42
CP_GSUB = 298
CP_POOLW = 426
CP_LI = 938
CP = 940
KP_PAR, KP_NPAR = 0, 1
KP_CNT = 2
KP_MC = 66
KP_ID = 72
KP = 200

EPOCH = 16000
ENGS = ("pe", "act", "dve", "pool", "sp")


class Obj:
    __slots__ = ("name", "w", "re", "rt", "const")

    def __init__(self, name, const=False):
        self.name = name
        self.w = None
        self.re = {}
        self.rt = {}
        self.const = const


class TL:
    def __init__(self, sem):
        self.sem = sem
        self.cnt = 0


class Prog:
    def __init__(self, nc, stack):
        self.nc = nc
        self.stack = stack
        self.dry = False
        self.q = {e: [] for e in ENGS}
        self.n = {e: 0 for e in ENGS}
        self.esems = {e: [] for e in ENGS}
        self.waited = {e: {} for e in ENGS}
        self.nsem = 0

    def new_sem(self, name):
        self.nsem += 1
        return self.stack.enter_context(self.nc.semaphore(name))

    def new_tl(self, name):
        return TL(self.new_sem(name))

    def _eng_sv(self, e, n):
        k = (n - 1) // EPOCH
        while len(self.esems[e]) <= k:
            self.esems[e].append(self.new_sem("e_%s_%d" % (e, len(self.esems[e]))))
        return self.esems[e][k], n - k * EPOCH

    def _resolve(self, eng, reads, writes, dma):
        need_e = {}
        need_t = {}

        def add(d):
            if d is None:
                return
            if d[0] == "eng":
                if need_e.get(d[1], 0) < d[2]:
                    need_e[d[1]] = d[2]
            else:
                if need_t.get(d[1], 0) < d[2]:
                    need_t[d[1]] = d[2]

        for o in reads:
            add(o.w)
        for o in writes:
            w = o.w
            if w is not None and not (w[0] == "eng" and w[1] == eng and eng == "pe" and not dma):
                add(w)
            for e2, n2 in o.re.items():
                if e2 == eng and eng == "pe" and not dma:
                    continue
                add(("eng", e2, n2))
            for t2, v2 in o.rt.items():
                add(("tl", t2, v2))
        waits = []
        wd = self.waited[eng]
        for e2, n2 in need_e.items():
            sem, val = self._eng_sv(e2, n2)
            key = id(sem)
            if wd.get(key, 0) < val:
                wd[key] = val
                waits.append((sem, val))
        for t2, v2 in need_t.items():
            v2 = t2.cnt
            key = id(t2.sem)
            if wd.get(key, 0) < v2:
                wd[key] = v2
                waits.append((t2.sem, v2))
        return waits

    def op(self, eng, fn, reads=(), writes=()):
        if self.dry:
            return
        waits = self._resolve(eng, reads, writes, False)
        self.n[eng] += 1
        n = self.n[eng]
        sem, _ = self._eng_sv(eng, n)
        for o in reads:
            if not o.const:
                o.re[eng] = n
        for o in writes:
            o.w = ("eng", eng, n)
            o.re = {}
            o.rt = {}
        self.q[eng].append((0, waits, fn, sem))

    def dma(self, eng, out_ap, in_ap, reads, writes, tl):
        if self.dry:
            return
        waits = self._resolve(eng, reads, writes, True)
        tl.cnt += 16
        val = tl.cnt
        for o in reads:
            if not o.const:
                o.rt[tl] = val
        for o in writes:
            o.w = ("tl", tl, val)
            o.re = {}
            o.rt = {}
        self.q[eng].append((1, waits, (out_ap, in_ap), tl.sem))

    def custom(self, eng, fn, reads, writes, tl, inc):
        if self.dry:
            return
        waits = self._resolve(eng, reads, writes, True)
        tl.cnt += inc
        val = tl.cnt
        for o in reads:
            if not o.const:
                o.rt[tl] = val
        for o in writes:
            o.w = ("tl", tl, val)
            o.re = {}
            o.rt = {}
        self.q[eng].append((2, waits, fn, (tl.sem, inc)))

    def wait_tl(self, eng, tl):
        if self.dry or tl.cnt == 0:
            return
        key = id(tl.sem)
        if self.waited[eng].get(key, 0) < tl.cnt:
            self.waited[eng][key] = tl.cnt
            self.q[eng].append((3, [(tl.sem, tl.cnt)], None, None))

    def final_wait(self, eng, tls):
        waits = []
        for tl in tls:
            if tl.cnt > 0:
                waits.append((tl.sem, tl.cnt))
        self.q[eng].append((3, waits, None, None))

    def replay(self, name, e):
        for kind, waits, a, b in self.q[name]:
            for s, v in waits:
                e.wait_ge(s, v)
            if kind == 0:
                a(e).then_inc(b, 1)
            elif kind == 1:
                e.dma_start(out=a[0], in_=a[1]).then_inc(b, 16)
            elif kind == 2:
                a(e).then_inc(b[0])


class Slot:
    __slots__ = ("t", "o")

    def __init__(self, t, o):
        self.t = t
        self.o = o


class Ring:
    def __init__(self, slots):
        self.slots = slots
        self.i = 0

    def next(self):
        s = self.slots[self.i % len(self.slots)]
        self.i += 1
        return s


class Stream:
    def __init__(self, pg, name, eng, nslots, maxpend=1):
        self.pg = pg
        self.eng = eng
        self.nslots = nslots
        self.maxpend = maxpend
        self.objs = [Obj("%s%d" % (name, i)) for i in range(nslots)]
        self.tls = [pg.new_tl("%s%d" % (name, i)) for i in range(nslots)]
        self.reqs = []
        self.groups = []
        self.idx = 0
        self.emitted = 0

    def reset(self):
        self.idx = 0
        self.emitted = 0

    def get(self, parts, group=0):
        n = self.idx
        self.idx += 1
        if self.pg.dry:
            self.reqs.append(parts)
            self.groups.append(group)
            return n % self.nslots
        lim = min(len(self.reqs), n + self.nslots - (self.maxpend - 1))
        while self.emitted < lim and self.groups[self.emitted] <= group:
            m = self.emitted
            sl = m % self.nslots
            for dstfn, src, rd in self.reqs[m]:
                self.pg.dma(self.eng, dstfn(sl), src, reads=rd, writes=[self.objs[sl]], tl=self.tls[sl])
            self.emitted += 1
        return n % self.nslots


class Ctx:
    pass


def lam_init_of(layer):
    return 0.8 - 0.6 * math.exp(-0.3 * layer)


def build_program(cfg):
    NT = cfg["NT"]
    layers = cfg["layers"]
    mode = cfg["mode"]
    L = len(layers)
    TOK = NT * T
    nc = bass.Bass("TRN2", target_bir_lowering=False)
    stack = contextlib.ExitStack()
    pg = Prog(nc, stack)
    c = Ctx()
    c.nc, c.NT, c.TOK, c.L, c.mode, c.layers = nc, NT, TOK, L, mode, layers
    c.B = cfg.get("B", 4)
    c.fake_cc = cfg.get("fake_cc", False)
    c.cc_serial = cfg.get("cc_serial", True)
    do_k = mode in ("K", "F")
    do_2 = mode in ("2", "F")

    def din(name, shape):
        return nc.dram_tensor(name, shape, F32, kind="ExternalInput").ap()

    def dout(name, shape):
        return nc.dram_tensor(name, shape, F32, kind="ExternalOutput").ap()

    def dint(name, shape):
        return nc.dram_tensor(name, shape, F32, kind="Internal").ap()

    c.hT_in = din("hT_in", [D, TOK])
    c.w_in = din("w_in", [L, D, DIN])
    c.cpack = din("cpack", [L, P, CP])
    c.kpack = din("kpack", [P, KP])
    if do_2:
        c.pT = din("pT", [L, PLE, TOK])
        c.w_pool_out = din("w_pool_out", [L, MIX, D])
        c.w_conv_out = din("w_conv_out", [L, MIX, D])
        c.w_attn_out = din("w_attn_out", [L, MIX, D])
        c.w_o = din("w_o", [L, D, D])
        c.w_up = din("w_up", [L, D, DFF])
        c.w_down = din("w_down", [L, DFF, D])
        c.w_ple_gate = din("w_ple_gate", [L, D, D])
        c.w_ple_proj = din("w_ple_proj", [L, PLE, D])
        c.hT_out = dout("hT_out", [D, TOK])
    HW = 8 * NT * HALO
    mk_loc = dout if mode == "K" else dint
    mk_all = din if mode == "2" else dint
    c.XK = [[mk_loc("xk%d_%d" % (l, i), [256, 1024]) if mode != "2" else None for i in range(NT)] for l in range(L)]
    c.XV = [[mk_loc("xv%d_%d" % (l, i), [256, 1024]) if mode != "2" else None for i in range(NT)] for l in range(L)]
    c.XH = [mk_loc("xh%d" % l, [P, HW]) if mode != "2" else None for l in range(L)]
    c.XKa = [[mk_all("xka%d_%d" % (l, i), [512, 1024]) if mode != "K" else None for i in range(NT)] for l in range(L)]
    c.XVa = [[mk_all("xva%d_%d" % (l, i), [512, 1024]) if mode != "K" else None for i in range(NT)] for l in range(L)]
    c.XHa = [mk_all("xha%d" % l, [2 * P, HW]) if mode != "K" else None for l in range(L)]
    if mode == "F":
        c.Hs = [dint("hs%d" % i, [D, TOK]) for i in range(L - 1)]

    c.o_xk = [[Obj("xk%d_%d" % (l, i)) for i in range(NT)] for l in range(L)]
    c.o_xv = [[Obj("xv%d_%d" % (l, i)) for i in range(NT)] for l in range(L)]
    c.o_xh = [Obj("xh%d" % l) for l in range(L)]
    c.o_xka = [[Obj("xka%d_%d" % (l, i)) for i in range(NT)] for l in range(L)]
    c.o_xva = [[Obj("xva%d_%d" % (l, i)) for i in range(NT)] for l in range(L)]
    c.o_xha = [Obj("xha%d" % l) for l in range(L)]
    c.o_hs = [[Obj("hs%d_%d" % (l, i)) for i in range(NT)] for l in range(L)]
    c.o_dconst = Obj("dconst", const=True)

    def sb(name, shape):
        return stack.enter_context(nc.sbuf_tensor(name, shape, F32))

    def ps(name):
        return stack.enter_context(nc.psum_tensor(name, [P, 512], F32))

    c.Hb = [sb("Hbuf%d" % k, [P, KC, T]) for k in range(2)]
    c.oHb = [[Obj("Hbuf%d_%d" % (k, i)) for i in range(KC)] for k in range(2)]
    c.H, c.oH = c.Hb[0], c.oHb[0]
    c.hsel = 0
    c.U = sb("U", [P, KC, T]); c.oU = [Obj("U%d" % i) for i in range(KC)]
    c.ones = sb("ones", [P, P]); c.blk = sb("blk", [P, P])
    c.eps = sb("eps", [P, 1])
    c.o_const = Obj("const", const=True)
    c.o_cm = Obj("constm", const=True)
    c.cp = [sb("cp%d" % l, [P, CP]) for l in range(L)]
    c.kp = sb("kp", [P, KP])
    c.lam = [sb("lam%d" % l, [P, 4]) for l in range(L)]
    c.gsubs = [sb("gsubs%d" % l, [P, P]) for l in range(L)]
    c.o_lam = [Obj("lam%d" % l) for l in range(L)]
    c.lamtmp = sb("lamtmp", [P, 64]); c.o_lamtmp = Obj("lamtmp")
    nta, ntb, ntg, ntm = 2, 2, 3, 2
    c.TA = [Slot(sb("TA%d" % i, [P, T]), Obj("TA%d" % i)) for i in range(nta)]
    c.TB = [Slot(sb("TB%d" % i, [P, T]), Obj("TB%d" % i)) for i in range(ntb)]
    c.TG = [Slot(sb("TG%d" % i, [P, T]), Obj("TG%d" % i)) for i in range(ntg)]
    c.TM = [Slot(sb("TM%d" % i, [P, T]), Obj("TM%d" % i)) for i in range(ntm)]
    NW = 8
    c.Wt = [sb("W%d" % i, [P, 1024]) for i in range(NW)]
    c.W = Stream(pg, "W", "sp", NW, maxpend=4)
    c.t_hb = [pg.new_tl("ldh0"), pg.new_tl("ldh1")]
    c.t_p = pg.new_tl("ldp")
    c.t_c = pg.new_tl("ldc")
    c.t_cc = pg.new_tl("cc")
    c.t_sk = pg.new_tl("stk"); c.t_sv = pg.new_tl("stv"); c.t_sx = pg.new_tl("stx"); c.t_so = pg.new_tl("sto")
    if do_k:
        c.UH = sb("UH", [P, KC, NT * HALO]); c.oUH = Obj("UH")
        c.HS = sb("HS", [P, 8, NT, HALO]); c.oHS = Obj("HS")
    if do_2:
        c.RY = sb("RY", [P, 16, T]); c.oRY = [Obj("RY%d" % i) for i in range(16)]
        c.M = sb("M", [P, KC, T]); c.oM = [Obj("M%d" % i) for i in range(KC)]
        c.PT = sb("PT", [P, 2, T]); c.oPT = Obj("PT")
        c.A0 = sb("A0", [P, HALO + T]); c.oA0 = Obj("A0")
        c.B1 = sb("B1", [P, HALO + T]); c.oB1 = Obj("B1")
        c.B2 = sb("B2", [P, HALO + T]); c.oB2 = Obj("B2")
        c.Z = sb("Z", [P, 2 + T]); c.oZ = Obj("Z")
        c.H0 = sb("H0", [P, 8, HALO]); c.oH0 = Obj("H0")
        c.H1 = sb("H1", [P, 8, HALO]); c.oH1 = Obj("H1")
        c.HB = sb("HB", [P, 8, HALO]); c.oHB = Obj("HB")
        NKV = 4
        c.Kt = [sb("Kt%d" % i, [P, T]) for i in range(NKV)]
        c.Vt = [sb("Vt%d" % i, [P, 4, 129]) for i in range(NKV)]
        c.KV = Stream(pg, "KV", "pool", NKV, maxpend=2)
        c.PB = [Slot(sb("PB%d" % i, [P, T]), Obj("PB%d" % i)) for i in range(4)]
        c.sm = [Slot(sb("sm%d" % i, [P, 8]), Obj("sm%d" % i)) for i in range(8)]
        c.ob = [Slot(sb("ob%d" % i, [P, P]), Obj("ob%d" % i)) for i in range(8)]
        c.jk = [Slot(sb("jk%d" % i, [P, P]), Obj("jk%d" % i)) for i in range(2)]
    c.PS = [Slot(ps("ps%d" % i), Obj("ps%d" % i)) for i in range(4)]
    c.OB = [Slot(ps("pob%d" % i), Obj("pob%d" % i)) for i in range(4)]

    for dry in (True, False):
        pg.dry = dry
        c.W.reset()
        if do_2:
            c.KV.reset()
        emit_all(c, pg)
    pg.dry = False
    tls = [c.t_sk, c.t_sv, c.t_sx, c.t_so]
    pg.final_wait("sp", tls)

    with nc.Block() as block:
        @block.tensor
        def _(e):
            pg.replay("pe", e)

        @block.scalar
        def _(e):
            pg.replay("act", e)

        @block.vector
        def _(e):
            pg.replay("dve", e)

        @block.gpsimd
        def _(e):
            pg.replay("pool", e)

        @block.sync
        def _(e):
            pg.replay("sp", e)

    stack.close()
    return nc, {e: len(pg.q[e]) for e in ENGS}


def emit_all(c, pg):
    nc = c.nc
    R = Ctx()
    R.ps = Ring(c.PS)
    R.ta, R.tb, R.tg, R.tm = Ring(c.TA), Ring(c.TB), Ring(c.TG), Ring(c.TM)
    if c.mode != "K":
        R.pb = Ring(c.PB); R.sm = Ring(c.sm); R.ob = Ring(c.ob); R.jk = Ring(c.jk)
    c.hsel = 0
    use_h(c, 0)
    emit_setup(c, pg, R)
    for li in range(c.L):
        lay = c.layers[li]
        hsrc = c.hT_in if (li == 0) else c.Hs[li - 1]
        hsrc_objs = None if li == 0 else c.o_hs[li - 1]
        if c.mode in ("K", "F"):
            emit_phase_k(c, pg, R, li, hsrc, hsrc_objs)
        if c.mode in ("2", "F"):
            last = (li == c.L - 1)
            hdst = c.hT_out if (last or c.mode == "2") else c.Hs[li]
            hdst_objs = c.o_hs[li]
            emit_phase_2(c, pg, R, li, lay, hsrc, hsrc_objs, hdst, hdst_objs)


def emit_setup(c, pg, R):
    nc = c.nc
    oc = c.o_const
    om = c.o_cm
    pg.op("dve", lambda e: e.memset(c.ones[:, :], 1.0), writes=[om])
    pg.op("dve", lambda e: e.memset(c.blk[:, :], 0.0), writes=[om])
    pg.op("dve", lambda e: e.memset(c.blk[0:64, 0:64], 1.0), writes=[om])
    pg.op("dve", lambda e: e.memset(c.blk[64:128, 64:128], 1.0), writes=[om])
    pg.op("dve", lambda e: e.memset(c.eps[:, :], EPS), writes=[om])
    for l in range(c.L):
        pg.dma("sp", c.cp[l][:, :], c.cpack[l], reads=[c.o_dconst], writes=[oc], tl=c.t_c)
    pg.dma("sp", c.kp[:, :], c.kpack, reads=[c.o_dconst], writes=[oc], tl=c.t_c)
    c.ident = c.kp[:, KP_ID:KP_ID + P]
    if c.mode != "K":
        for i in range(len(c.Vt)):
            pg.op("dve", lambda e, i=i: e.memset(c.Vt[i][:, :, 128:129], 1.0), writes=[c.KV.objs[i]])
        for l in range(c.L):
            cp = c.cp[l]
            lam = c.lam[l]
            ol = c.o_lam[l]
            for k in range(2):
                a = CP_LAM + (2 * k) * 64
                b = CP_LAM + (2 * k + 1) * 64
                pg.op("dve", lambda e, a=a, b=b, cp=cp: e.tensor_tensor(out=c.lamtmp[:, :], in0=cp[:, a:a + 64], in1=cp[:, b:b + 64], op=ALU.mult),
                      reads=[oc, c.o_lamtmp], writes=[c.o_lamtmp])
                pg.op("dve", lambda e, k=k, lam=lam: e.reduce_sum(out=lam[:, k:k + 1], in_=c.lamtmp[:, :], axis=AX.X),
                      reads=[c.o_lamtmp, ol], writes=[ol])
            pg.op("act", lambda e, lam=lam: e.activation(out=lam[:, 0:2], in_=lam[:, 0:2], func=AF.Exp), reads=[ol], writes=[ol])
            pg.op("dve", lambda e, lam=lam: e.tensor_tensor(out=lam[:, 2:3], in0=lam[:, 1:2], in1=lam[:, 0:1], op=ALU.subtract),
                  reads=[ol], writes=[ol])
            pg.op("dve", lambda e, lam=lam, cp=cp: e.tensor_tensor(out=lam[:, 3:4], in0=lam[:, 2:3], in1=cp[:, CP_LI:CP_LI + 1], op=ALU.add),
                  reads=[ol, oc], writes=[ol])
            pg.op("dve", lambda e, l=l, cp=cp: e.tensor_scalar(out=c.gsubs[l][:, :], in0=cp[:, CP_GSUB:CP_GSUB + P], scalar1=cp[:, CP_LI + 1:CP_LI + 2], scalar2=None, op0=ALU.mult),
                  reads=[oc, ol], writes=[ol])


def load_h(c, pg, hsrc, hsrc_objs, i, k):
    src = hsrc.rearrange("(kc p) t -> p kc t", p=P)[:, :, i * T:(i + 1) * T]
    rd = [c.o_dconst] if hsrc_objs is None else [hsrc_objs[i]]
    half = KC // 2
    Hk, oHk = c.Hb[k], c.oHb[k]
    pg.dma("sp", Hk[:, 0:half, :], src[:, 0:half, :], reads=rd, writes=oHk[0:half], tl=c.t_hb[k])
    pg.dma("sp", Hk[:, half:KC, :], src[:, half:KC, :], reads=rd, writes=oHk[half:KC], tl=c.t_hb[k])


def use_h(c, k):
    c.H, c.oH = c.Hb[k], c.oHb[k]


def emit_norm(c, pg, R, gcol0, cp):
    ss = R.ps.next()
    for ch in range(KC):
        sq = R.ta.next()
        pg.op("act", lambda e, sq=sq, ch=ch, H=c.H: e.activation(out=sq.t[:, :], in_=H[:, ch, :], func=AF.Square),
              reads=[c.oH[ch]], writes=[sq.o])
        pg.op("pe", lambda e, sq=sq, ch=ch, ss=ss: e.matmul(ss.t[:, :], lhsT=c.ones[:, :], rhs=sq.t[:, :], start=(ch == 0), stop=(ch == KC - 1)),
              reads=[sq.o, c.o_cm], writes=[ss.o])
    rs = R.tb.next()
    pg.op("act", lambda e, rs=rs, ss=ss: e.activation(out=rs.t[:, :], in_=ss.t[:, :], func=AF.Sqrt, bias=c.eps[:, 0:1], scale=1.0 / D),
          reads=[ss.o, c.o_cm], writes=[rs.o])
    pg.op("dve", lambda e, rs=rs: e.reciprocal(out=rs.t[:, :], in_=rs.t[:, :]), reads=[rs.o], writes=[rs.o])
    for ch in range(KC):
        pg.op("dve", lambda e, ch=ch, rs=rs, H=c.H: e.scalar_tensor_tensor(out=c.U[:, ch, :], in0=H[:, ch, :], scalar=cp[:, gcol0 + ch:gcol0 + ch + 1],
                                                                   in1=rs.t[:, :], op0=ALU.mult, op1=ALU.mult),
              reads=[c.oH[ch], rs.o, c.o_const], writes=[c.oU[ch]])


def wreq(c, src3, kcn, ncol):
    def dst(sl):
        return c.Wt[sl][:, 0:kcn * ncol].rearrange("p (k n) -> p k n", k=kcn)
    sl = c.W.get([(dst, src3, [c.o_dconst])])
    return dst(sl), c.W.objs[sl]


def proj_u(c, pg, R, w2d, col0, ncol=P, n_tok=None, rhs_fn=None, rhs_objs=None, kcn=KC):
    src = w2d.rearrange("(kc p) n -> p kc n", p=P)[:, :, col0:col0 + ncol]
    wv, wo = wreq(c, src, kcn, ncol)
    out = R.ps.next()
    if rhs_fn is None:
        rhs_fn = lambda kc: c.U[:, kc, :]
        rhs_objs = c.oU
    nt = T if n_tok is None else n_tok
    for kc in range(kcn):
        pg.op("pe", lambda e, kc=kc, out=out, wv=wv: e.matmul(out.t[:, 0:nt], lhsT=wv[:, kc, :], rhs=rhs_fn(kc), start=(kc == 0), stop=(kc == kcn - 1)),
              reads=[wo, rhs_objs[kc]], writes=[out.o])
    return out


def qk_norm(c, pg, R, xps, gcol_ap, out_ap, out_obj):
    sq = R.ta.next()
    pg.op("act", lambda e: e.activation(out=sq.t[:, :], in_=xps.t[:, :], func=AF.Square), reads=[xps.o], writes=[sq.o])
    ss = R.ps.next()
    pg.op("pe", lambda e: e.matmul(ss.t[:, :], lhsT=c.blk[:, :], rhs=sq.t[:, :], start=True, stop=True),
          reads=[sq.o, c.o_cm], writes=[ss.o])
    rs = R.tb.next()
    pg.op("act", lambda e: e.activation(out=rs.t[:, :], in_=ss.t[:, :], func=AF.Sqrt, bias=c.eps[:, 0:1], scale=1.0 / 64),
          reads=[ss.o, c.o_cm], writes=[rs.o])
    pg.op("dve", lambda e: e.reciprocal(out=rs.t[:, :], in_=rs.t[:, :]), reads=[rs.o], writes=[rs.o])
    pg.op("dve", lambda e: e.scalar_tensor_tensor(out=out_ap, in0=xps.t[:, :], scalar=gcol_ap, in1=rs.t[:, :], op0=ALU.mult, op1=ALU.mult),
          reads=[xps.o, rs.o, c.o_const], writes=[out_obj])


def v512(ap2):
    return ap2.rearrange("a (b c) -> (a b) c", b=2)


def emit_phase_k(c, pg, R, li, hsrc, hsrc_objs):
    NT = c.NT
    cp = c.cp[li]
    w_in = c.w_in[li]
    load_h(c, pg, hsrc, hsrc_objs, 0, c.hsel)
    for i in range(NT):
        if c.mode == "F" and i >= 1:
            emit_gather_tile(c, pg, li, i - 1)
        use_h(c, c.hsel)
        if i + 1 < NT:
            load_h(c, pg, hsrc, hsrc_objs, i + 1, 1 - c.hsel)
        c.hsel = 1 - c.hsel
        emit_norm(c, pg, R, CP_GMIX, cp)
        pg.op("pool", lambda e, i=i: e.tensor_copy(out=c.UH[:, :, i * HALO:(i + 1) * HALO], in_=c.U[:, :, T - HALO:T]),
              reads=c.oU, writes=[c.oUH])
        for j in range(4):
            kps = proj_u(c, pg, R, w_in, C_K + j * P)
            kn = R.tg.next()
            qk_norm(c, pg, R, kps, cp[:, CP_GK:CP_GK + 1], kn.t[:, :], kn.o)
            pg.dma("sp", v512(c.XK[li][i])[j * P:(j + 1) * P, :], kn.t[:, :], reads=[kn.o], writes=[c.o_xk[li][i]], tl=c.t_sk)
        wsl = []
        for k2 in range(4):
            src = w_in.rearrange("(kc p) n -> p kc n", p=P)[:, 2 * k2:2 * k2 + 2, C_V:C_V + 512]
            wsl.append(wreq(c, src, 2, 512))
        for tb in range(4):
            vps = R.ps.next()
            for kc in range(KC):
                wv, wo = wsl[kc // 2]
                pg.op("pe", lambda e, kc=kc, tb=tb, vps=vps, wv=wv: e.matmul(vps.t[:, :], lhsT=c.U[:, kc, tb * P:(tb + 1) * P], rhs=wv[:, kc % 2, :],
                                                                              start=(kc == 0), stop=(kc == KC - 1)),
                      reads=[wo, c.oU[kc]], writes=[vps.o])
            vs = R.tg.next()
            pg.op("act", lambda e, vs=vs, vps=vps: e.activation(out=vs.t[:, :], in_=vps.t[:, :], func=AF.Copy), reads=[vps.o], writes=[vs.o])
            pg.dma("sp", v512(c.XV[li][i])[tb * P:(tb + 1) * P, :], vs.t[:, :], reads=[vs.o], writes=[c.o_xv[li][i]], tl=c.t_sv)
    NH = NT * HALO
    uh_fn = lambda kc: c.UH[:, kc, :]
    uh_objs = [c.oUH] * KC
    hs2 = c.HS[:, :, :, :].rearrange("p a b t -> p a (b t)")
    for j in range(4):
        aps = proj_u(c, pg, R, w_in, C_A + j * P, n_tok=NH, rhs_fn=uh_fn, rhs_objs=uh_objs)
        pg.op("act", lambda e, j=j, aps=aps: e.activation(out=hs2[:, j, :], in_=aps.t[:, 0:NH], func=AF.Copy), reads=[aps.o], writes=[c.oHS])
    for j in range(4):
        xps = proj_u(c, pg, R, w_in, C_CX + j * P, n_tok=NH, rhs_fn=uh_fn, rhs_objs=uh_objs)
        xs = R.tm.next()
        pg.op("act", lambda e, xs=xs, xps=xps: e.activation(out=xs.t[:, 0:NH], in_=xps.t[:, 0:NH], func=AF.Copy), reads=[xps.o], writes=[xs.o])
        cps = proj_u(c, pg, R, w_in, C_CC + j * P, n_tok=NH, rhs_fn=uh_fn, rhs_objs=uh_objs)
        pg.op("dve", lambda e, j=j, xs=xs, cps=cps: e.tensor_tensor(out=hs2[:, 4 + j, :], in0=cps.t[:, 0:NH], in1=xs.t[:, 0:NH], op=ALU.mult),
              reads=[cps.o, xs.o], writes=[c.oHS])
    pg.dma("sp", c.XH[li], c.HS[:, :, :, :].rearrange("p a b t -> p (a b t)"), reads=[c.oHS], writes=[c.o_xh[li]], tl=c.t_sx)
    if c.mode == "F":
        emit_gather_tile(c, pg, li, NT - 1)
        groups = [[2 * b, 2 * b + 1] for b in range(c.B)]
        xh, xha = c.XH[li], c.XHa[li]
        if c.fake_cc:
            for r in range(2):
                pg.dma("pool", xha[r * P:(r + 1) * P, :], xh, reads=[c.o_xh[li]], writes=[c.o_xha[li]], tl=c.t_cc)
        else:
          pg.custom("pool", lambda e: e.collective_compute("AllGather", ALU.bypass, replica_groups=groups, ins=[xh.opt()], outs=[xha.opt()]),
                  reads=[c.o_xh[li]], writes=[c.o_xha[li]], tl=c.t_cc, inc=1)
          if c.cc_serial:
            pg.wait_tl("pool", c.t_cc)


def emit_gather_tile(c, pg, li, i):
    groups = [[2 * b, 2 * b + 1] for b in range(c.B)]
    if c.fake_cc:
        for src, dst, osrc, odst in ((c.XK[li][i], c.XKa[li][i], c.o_xk[li][i], c.o_xka[li][i]),
                                     (c.XV[li][i], c.XVa[li][i], c.o_xv[li][i], c.o_xva[li][i])):
            for r in range(2):
                pg.dma("pool", dst[r * 256:(r + 1) * 256, :], src, reads=[osrc], writes=[odst], tl=c.t_cc)
        return
    for src, dst, osrc, odst in ((c.XK[li][i], c.XKa[li][i], c.o_xk[li][i], c.o_xka[li][i]),
                                 (c.XV[li][i], c.XVa[li][i], c.o_xv[li][i], c.o_xva[li][i])):
        pg.custom("pool", lambda e, src=src, dst=dst: e.collective_compute("AllGather", ALU.bypass, replica_groups=groups,
                                                                         ins=[src.opt()], outs=[dst.opt()]),
                  reads=[osrc], writes=[odst], tl=c.t_cc, inc=1)
        if c.cc_serial:
            pg.wait_tl("pool", c.t_cc)


def emit_phase_2(c, pg, R, li, lay, hsrc, hsrc_objs, hdst, hdst_objs):
    NT = c.NT
    cp = c.cp[li]
    kp = c.kp
    w_in = c.w_in[li]
    fz = (c.mode == "F")
    HW = 8 * NT * HALO
    oxha = c.o_xha[li] if fz else c.o_dconst
    hviews = [c.XHa[li][r * P:(r + 1) * P, :].rearrange("p (a b t) -> p a b t", a=8, b=NT) for r in range(2)]
    lam = c.lam[li]
    ol = c.o_lam[li]
    YA = lambda j: c.RY[:, j, :]
    YB = lambda j: c.RY[:, 4 + j, :]
    YC = lambda j: c.RY[:, 8 + j, :]
    QN = lambda j: c.RY[:, 12 + j, :]
    oYA, oYB, oYC, oQN = c.oRY[0:4], c.oRY[4:8], c.oRY[8:12], c.oRY[12:16]
    oc = c.o_const
    load_h(c, pg, hsrc, hsrc_objs, 0, c.hsel)
    for i in range(NT):
        use_h(c, c.hsel)
        c.hsel = 1 - c.hsel
        pg.dma("sp", c.PT[:, :, :], c.pT[li].rearrange("(k p) t -> p k t", p=P)[:, :, i * T:(i + 1) * T],
               reads=[c.o_dconst], writes=[c.oPT], tl=c.t_p)
        h0v = hviews[0][:, :, i, :]
        pg.dma("sp", c.H0[:, :, :], h0v, reads=[oxha], writes=[c.oH0], tl=c.t_p)
        if i > 0:
            h1v = hviews[1][:, :, i - 1, :]
            pg.dma("sp", c.H1[:, :, :], h1v, reads=[oxha], writes=[c.oH1], tl=c.t_p)
        else:
            pg.op("dve", lambda e: e.memset(c.H1[:, :, :], 0.0), writes=[c.oH1])
        pg.op("dve", lambda e: e.tensor_scalar(out=c.HB[:, :, :], in0=c.H0[:, :, :], scalar1=kp[:, KP_PAR:KP_PAR + 1], scalar2=None, op0=ALU.mult),
              reads=[c.oH0, oc], writes=[c.oHB])
        pg.op("dve", lambda e: e.scalar_tensor_tensor(out=c.HB[:, :, :], in0=c.H1[:, :, :], scalar=kp[:, KP_NPAR:KP_NPAR + 1], in1=c.HB[:, :, :],
                                                      op0=ALU.mult, op1=ALU.add),
              reads=[c.oH1, c.oHB, oc], writes=[c.oHB])
        emit_norm(c, pg, R, CP_GMIX, cp)

        for g in range(4):
            aps = proj_u(c, pg, R, w_in, C_A + g * P)
            pg.op("dve", lambda e, g=g: e.tensor_copy(out=c.A0[:, 0:HALO], in_=c.HB[:, g, :]), reads=[c.oHB], writes=[c.oA0])
            pg.op("act", lambda e, aps=aps: e.activation(out=c.A0[:, HALO:HALO + T], in_=aps.t[:, :], func=AF.Copy), reads=[aps.o], writes=[c.oA0])
            cur, ocur = c.A0, c.oA0
            bufs = [(c.B1, c.oB1), (c.B2, c.oB2)]
            sh = 1
            lo = 0
            for st in range(g + 1):
                nb, onb = bufs[st % 2]
                lo = lo + sh
                W_ = HALO + T
                pg.op("dve", lambda e, cur=cur, nb=nb, lo=lo, sh=sh, W_=W_: e.tensor_tensor(out=nb[:, lo:W_], in0=cur[:, lo:W_], in1=cur[:, lo - sh:W_ - sh], op=ALU.add),
                      reads=[ocur], writes=[onb])
                cur, ocur = nb, onb
                sh *= 2
            win = float(2 ** (g + 1))
            pl = R.tm.next()
            pg.op("dve", lambda e, cur=cur, pl=pl, win=win: e.scalar_tensor_tensor(out=pl.t[:, :], in0=cur[:, HALO:HALO + T], scalar=1.0 / win, in1=c.A0[:, HALO:HALO + T],
                                                                                 op0=ALU.mult, op1=ALU.subtract),
                  reads=[ocur, c.oA0], writes=[pl.o])
            if i == 0:
                pg.op("dve", lambda e, cur=cur, pl=pl, g=g: e.tensor_tensor(out=pl.t[:, 0:HALO], in0=cur[:, HALO:2 * HALO],
                                                                           in1=kp[:, KP_CNT + g * HALO:KP_CNT + (g + 1) * HALO], op=ALU.mult),
                      reads=[ocur, oc, pl.o], writes=[pl.o])
                pg.op("dve", lambda e, pl=pl: e.tensor_tensor(out=pl.t[:, 0:HALO], in0=pl.t[:, 0:HALO], in1=c.A0[:, HALO:2 * HALO], op=ALU.subtract),
                      reads=[pl.o, c.oA0], writes=[pl.o])
            mps = R.ps.next()
            pg.op("pe", lambda e, g=g, pl=pl, mps=mps: e.matmul(mps.t[:, :], lhsT=cp[:, CP_POOLW + g * P:CP_POOLW + (g + 1) * P], rhs=pl.t[:, :], start=True, stop=True),
                  reads=[pl.o, oc], writes=[mps.o])
            pg.op("act", lambda e, g=g, mps=mps: e.activation(out=YA(g), in_=mps.t[:, :], func=AF.Copy, scale=cp[:, CP_PSC + g:CP_PSC + g + 1]),
                  reads=[mps.o, oc], writes=[oYA[g]])

        for j in range(4):
            xps = proj_u(c, pg, R, w_in, C_CX + j * P)
            xs = R.tg.next()
            pg.op("act", lambda e, xs=xs, xps=xps: e.activation(out=xs.t[:, :], in_=xps.t[:, :], func=AF.Copy), reads=[xps.o], writes=[xs.o])
            cps = proj_u(c, pg, R, w_in, C_CC + j * P)
            pg.op("dve", lambda e, j=j: e.tensor_copy(out=c.Z[:, 0:2], in_=c.HB[:, 4 + j, HALO - 2:HALO]), reads=[c.oHB], writes=[c.oZ])
            pg.op("dve", lambda e, xs=xs, cps=cps: e.tensor_tensor(out=c.Z[:, 2:2 + T], in0=cps.t[:, :], in1=xs.t[:, :], op=ALU.mult),
                  reads=[cps.o, xs.o], writes=[c.oZ])
            bps = proj_u(c, pg, R, w_in, C_CB + j * P)
            tc = R.tm.next()
            cw = lambda k, j=j: cp[:, CP_CONV + k * 4 + j:CP_CONV + k * 4 + j + 1]
            pg.op("dve", lambda e, tc=tc, cw=cw: e.tensor_scalar(out=tc.t[:, :], in0=c.Z[:, 0:T], scalar1=cw(0), scalar2=None, op0=ALU.mult),
                  reads=[c.oZ, oc], writes=[tc.o])
            pg.op("dve", lambda e, tc=tc, cw=cw: e.scalar_tensor_tensor(out=tc.t[:, :], in0=c.Z[:, 1:1 + T], scalar=cw(1), in1=tc.t[:, :], op0=ALU.mult, op1=ALU.add),
                  reads=[c.oZ, tc.o, oc], writes=[tc.o])
            pg.op("dve", lambda e, tc=tc, cw=cw: e.scalar_tensor_tensor(out=tc.t[:, :], in0=c.Z[:, 2:2 + T], scalar=cw(2), in1=tc.t[:, :], op0=ALU.mult, op1=ALU.add),
                  reads=[c.oZ, tc.o, oc], writes=[tc.o])
            pg.op("dve", lambda e, j=j, tc=tc, bps=bps: e.tensor_tensor(out=YB(j), in0=bps.t[:, :], in1=tc.t[:, :], op=ALU.mult),
                  reads=[bps.o, tc.o], writes=[oYB[j]])

        for h in range(4):
            qps = proj_u(c, pg, R, w_in, C_Q + h * P)
            qk_norm(c, pg, R, qps, cp[:, CP_GQ:CP_GQ + 1], QN(h), oQN[h])
        nkt = 2 * i + 2
        seq = [(h, gk, kb) for h in range(4) for gk in range(nkt) for kb in range(4)]
        kvcur = {}

        def att_S(h, gk, kb):
            if kb == 0:
                rank, lt = gk % 2, gk // 2
                ksrc = v512(c.XKa[li][lt][rank * 256:(rank + 1) * 256, :])[h * P:(h + 1) * P, :]
                vsrc = v512(c.XVa[li][lt][rank * 256:(rank + 1) * 256, :])[:, h * P:(h + 1) * P].rearrange("(kb p) d -> p kb d", p=P)
                oka = c.o_xka[li][lt] if fz else c.o_dconst
                ova = c.o_xva[li][lt] if fz else c.o_dconst
                sl = c.KV.get([(lambda s: c.Kt[s][:, :], ksrc, [oka]), (lambda s: c.Vt[s][:, :, 0:P], vsrc, [ova])], group=li)
                kvcur[(h, gk)] = sl
            sl = kvcur[(h, gk)]
            okv = c.KV.objs[sl]
            Kt, Vt = c.Kt[sl], c.Vt[sl]
            slot_m = gk - 2 * i
            pbs = []
            for w in range(2):
                sps = R.ps.next()
                p0 = 64 * w
                pg.op("pe", lambda e, sps=sps, p0=p0, kb=kb, Kt=Kt, h=h: e.matmul(sps.t[:, :], lhsT=Kt[p0:p0 + 64, kb * P:(kb + 1) * P],
                                                                               rhs=c.RY[p0:p0 + 64, 12 + h, :], start=True, stop=True),
                      reads=[okv, oQN[h]], writes=[sps.o])
                pb = R.pb.next()
                pg.op("act", lambda e, pb=pb, sps=sps: e.activation(out=pb.t[:, :], in_=sps.t[:, :], func=AF.Exp, scale=0.125),
                      reads=[sps.o], writes=[pb.o])
                if slot_m >= 0:
                    ranges = [(0, kb * P), (kb * P, kb * P + 64), (kb * P + 64, T)]
                    for ri, (a_, b_) in enumerate(ranges):
                        if b_ <= a_:
                            continue
                        col = KP_MC + slot_m * 3 + ri
                        pg.op("dve", lambda e, pb=pb, a_=a_, b_=b_, col=col: e.tensor_scalar(out=pb.t[:, a_:b_], in0=pb.t[:, a_:b_], scalar1=kp[:, col:col + 1],
                                                                                         scalar2=None, op0=ALU.mult),
                              reads=[pb.o, oc], writes=[pb.o])
                pbs.append(pb)
            return pbs, Vt, okv

        def att_PV(h, gk, kb, info):
            pbs, Vt, okv = info
            first = (gk == 0 and kb == 0)
            last = (gk == nkt - 1 and kb == 3)
            slot_m = gk - 2 * i
            for qb in range(4):
                if slot_m == 1 and qb < kb:
                    continue
                for w in range(2):
                    a_i = qb * 2 + w
                    bank, pos = a_i // 3, a_i % 3
                    ob = c.OB[bank]
                    st = first and pos == 0
                    pg.op("pe", lambda e, ob=ob, pos=pos, pb=pbs[w], qb=qb, Vt=Vt, kb=kb, st=st, last=last:
                          e.matmul(ob.t[:, pos * 129:(pos + 1) * 129], lhsT=pb.t[:, qb * P:(qb + 1) * P], rhs=Vt[:, kb, :],
                                   start=st, stop=last, skip_group_check=True),
                          reads=[pbs[w].o, okv], writes=[ob.o])

        def att_fin_pre(h):
            outs = []
            for qb in range(4):
                acc = []
                for w in range(2):
                    a_i = qb * 2 + w
                    bank, pos = a_i // 3, a_i % 3
                    acc.append((c.OB[bank], pos * 129))
                sm = R.sm.next()
                o1 = R.ob.next()
                (b1, c1), (b2, c2) = acc
                pg.op("dve", lambda e, sm=sm, b1=b1, c1=c1: e.reciprocal(out=sm.t[:, 0:1], in_=b1.t[:, c1 + 128:c1 + 129]), reads=[b1.o], writes=[sm.o])
                pg.op("dve", lambda e, sm=sm, b2=b2, c2=c2: e.reciprocal(out=sm.t[:, 1:2], in_=b2.t[:, c2 + 128:c2 + 129]), reads=[b2.o, sm.o], writes=[sm.o])
                pg.op("dve", lambda e, sm=sm: e.tensor_tensor(out=sm.t[:, 2:3], in0=sm.t[:, 1:2], in1=lam[:, 3:4], op=ALU.mult), reads=[sm.o, ol], writes=[sm.o])
                pg.op("dve", lambda e, sm=sm, o1=o1, b1=b1, c1=c1: e.tensor_scalar(out=o1.t[:, :], in0=b1.t[:, c1:c1 + 128], scalar1=sm.t[:, 0:1], scalar2=None, op0=ALU.mult),
                      reads=[b1.o, sm.o], writes=[o1.o])
                pg.op("dve", lambda e, sm=sm, o1=o1, b2=b2, c2=c2: e.scalar_tensor_tensor(out=o1.t[:, :], in0=b2.t[:, c2:c2 + 128], scalar=sm.t[:, 2:3], in1=o1.t[:, :],
                                                                                           op0=ALU.mult, op1=ALU.add),
                      reads=[b2.o, sm.o, o1.o], writes=[o1.o])
                jk = R.jk.next()
                pg.op("act", lambda e, o1=o1, jk=jk: e.activation(out=jk.t[:, :], in_=o1.t[:, :], func=AF.Square),
                      reads=[o1.o], writes=[jk.o])
                pg.op("dve", lambda e, sm=sm, jk=jk: e.reduce_sum(out=sm.t[:, 3:4], in_=jk.t[:, :], axis=AX.X), reads=[jk.o, sm.o], writes=[sm.o])
                pg.op("act", lambda e, sm=sm: e.activation(out=sm.t[:, 4:5], in_=sm.t[:, 3:4], func=AF.Sqrt, bias=c.eps[:, 0:1], scale=1.0 / 128),
                      reads=[sm.o, c.o_cm], writes=[sm.o])
                pg.op("dve", lambda e, sm=sm: e.reciprocal(out=sm.t[:, 5:6], in_=sm.t[:, 4:5]), reads=[sm.o], writes=[sm.o])
                pg.op("dve", lambda e, sm=sm, o1=o1: e.scalar_tensor_tensor(out=o1.t[:, :], in0=o1.t[:, :], scalar=sm.t[:, 5:6], in1=c.gsubs[li][:, :], op0=ALU.mult, op1=ALU.mult),
                      reads=[o1.o, sm.o, ol], writes=[o1.o])
                outs.append((o1, qb))
            return h, outs

        def att_fin_T(fin):
            h, outs = fin
            for o1, qb in outs:
                tps = R.ps.next()
                pg.op("pe", lambda e, tps=tps, o1=o1: e.transpose(tps.t[:, 0:P], o1.t[:, :], c.ident[:, :]), reads=[o1.o, oc], writes=[tps.o])
                pg.op("act", lambda e, tps=tps, h=h, qb=qb: e.activation(out=c.RY[:, 8 + h, qb * P:(qb + 1) * P], in_=tps.t[:, 0:P], func=AF.Copy),
                      reads=[tps.o], writes=[oYC[h]])

        infos = {0: att_S(*seq[0])}
        pending_T = None
        since = 0
        for idx in range(len(seq)):
            if idx + 1 < len(seq):
                infos[idx + 1] = att_S(*seq[idx + 1])
            h, gk, kb = seq[idx]
            att_PV(h, gk, kb, infos.pop(idx))
            since += 1
            if pending_T is not None and since == 3:
                att_fin_T(pending_T)
                pending_T = None
            if gk == nkt - 1 and kb == 3:
                assert pending_T is None
                pending_T = att_fin_pre(h)
                since = 0

        if i + 1 < NT:
            load_h(c, pg, hsrc, hsrc_objs, i + 1, c.hsel)
        wbr = [(c.w_pool_out[li], YA, oYA), (c.w_conv_out[li], YB, oYB), (c.w_attn_out[li], YC, oYC)]
        for j in range(KC):
            gs = []
            for b in range(3):
                gps = proj_u(c, pg, R, w_in, C_G + b * D + j * P)
                g = R.tg.next()
                pg.op("act", lambda e, g=g, gps=gps: e.activation(out=g.t[:, :], in_=gps.t[:, :], func=AF.Sigmoid), reads=[gps.o], writes=[g.o])
                gs.append(g)
            if pending_T is not None:
                att_fin_T(pending_T)
                pending_T = None
            for b in range(3):
                w2d, yf, yo = wbr[b]
                pps = proj_u(c, pg, R, w2d, j * P, rhs_fn=yf, rhs_objs=yo, kcn=4)
                if b == 0:
                    pg.op("dve", lambda e, j=j, g=gs[b], pps=pps: e.tensor_tensor(out=c.M[:, j, :], in0=pps.t[:, :], in1=g.t[:, :], op=ALU.mult),
                          reads=[pps.o, gs[b].o], writes=[c.oM[j]])
                else:
                    pg.op("dve", lambda e, g=gs[b], pps=pps: e.tensor_tensor(out=g.t[:, :], in0=pps.t[:, :], in1=g.t[:, :], op=ALU.mult),
                          reads=[pps.o, gs[b].o], writes=[gs[b].o])
                    pg.op("pool", lambda e, j=j, g=gs[b]: e.tensor_tensor(out=c.M[:, j, :], in0=c.M[:, j, :], in1=g.t[:, :], op=ALU.add),
                          reads=[c.oM[j], gs[b].o], writes=[c.oM[j]])
        mfn = lambda kc: c.M[:, kc, :]
        for j in range(KC):
            ops_ = proj_u(c, pg, R, c.w_o[li], j * P, rhs_fn=mfn, rhs_objs=c.oM)
            pg.op("dve", lambda e, j=j, ops_=ops_, H=c.H: e.tensor_tensor(out=H[:, j, :], in0=ops_.t[:, :], in1=H[:, j, :], op=ALU.add),
                  reads=[ops_.o, c.oH[j]], writes=[c.oH[j]])

        emit_norm(c, pg, R, CP_GMLP, cp)
        rfn = lambda f: c.RY[:, f, :]
        for fh in range(2):
            for f in range(16):
                ups = proj_u(c, pg, R, c.w_up[li], (fh * 16 + f) * P)
                pg.op("act", lambda e, f=f, ups=ups: e.activation(out=c.RY[:, f, :], in_=ups.t[:, :], func=AF.Relu), reads=[ups.o], writes=[c.oRY[f]])
                pg.op("pool", lambda e, f=f: e.tensor_tensor(out=c.RY[:, f, :], in0=c.RY[:, f, :], in1=c.RY[:, f, :], op=ALU.mult),
                      reads=[c.oRY[f]], writes=[c.oRY[f]])
            for chf in range(2):
                for f2 in range(8):
                    r0 = (fh * 16 + 2 * f2) * P
                    src = c.w_down[li][r0:r0 + 2 * P, chf * 512:(chf + 1) * 512].rearrange("(k p) n -> p k n", p=P)
                    wv, wo = wreq(c, src, 2, 512)
                    for k in range(2):
                        f = 2 * f2 + k
                        for j in range(4):
                            ob = c.OB[j]
                            pg.op("pe", lambda e, ob=ob, wv=wv, k=k, j=j, f=f: e.matmul(ob.t[:, :], lhsT=wv[:, k, j * P:(j + 1) * P], rhs=c.RY[:, f, :],
                                                                                       start=(f == 0), stop=(f == 15)),
                                  reads=[wo, c.oRY[f]], writes=[ob.o])
                for j in range(4):
                    jj = chf * 4 + j
                    ob = c.OB[j]
                    pg.op("dve", lambda e, jj=jj, ob=ob, H=c.H: e.tensor_tensor(out=H[:, jj, :], in0=ob.t[:, :], in1=H[:, jj, :], op=ALU.add),
                          reads=[ob.o, c.oH[jj]], writes=[c.oH[jj]])

        emit_norm(c, pg, R, CP_GPLE, cp)
        pfn = lambda k: c.PT[:, k, :]
        for j in range(KC):
            gps = proj_u(c, pg, R, c.w_ple_gate[li], j * P)
            g = R.tg.next()
            pg.op("act", lambda e, g=g, gps=gps: e.activation(out=g.t[:, :], in_=gps.t[:, :], func=AF.Sigmoid), reads=[gps.o], writes=[g.o])
            pps = proj_u(c, pg, R, c.w_ple_proj[li], j * P, rhs_fn=pfn, rhs_objs=[c.oPT, c.oPT], kcn=2)
            pg.op("dve", lambda e, g=g, pps=pps: e.tensor_tensor(out=g.t[:, :], in0=pps.t[:, :], in1=g.t[:, :], op=ALU.mult),
                  reads=[pps.o, g.o], writes=[g.o])
            pg.op("pool", lambda e, j=j, g=g, H=c.H: e.tensor_tensor(out=H[:, j, :], in0=H[:, j, :], in1=g.t[:, :], op=ALU.add),
                  reads=[c.oH[j], g.o], writes=[c.oH[j]])
        dst = hdst.rearrange("(kc p) t -> p kc t", p=P)[:, :, i * T:(i + 1) * T]
        half = KC // 2
        pg.dma("sp", dst[:, 0:half, :], c.H[:, 0:half, :], reads=c.oH[0:half], writes=[hdst_objs[i]], tl=c.t_so)
        pg.dma("sp", dst[:, half:KC, :], c.H[:, half:KC, :], reads=c.oH[half:KC], writes=[hdst_objs[i]], tl=c.t_so)


def tok_index(NT, r):
    idx = []
    for i in range(NT):
        g = 2 * i + r
        idx.append(np.arange(g * T, (g + 1) * T))
    return np.concatenate(idx)


def make_cpack(inp, l):
    cpk = np.zeros((P, CP), np.float32)
    f32 = lambda a: np.asarray(a, np.float32)
    cpk[:, CP_GMIX:CP_GMIX + 8] = f32(inp["norm_mix_g"][l]).reshape(8, P).T
    cpk[:, CP_GMLP:CP_GMLP + 8] = f32(inp["norm_mlp_g"][l]).reshape(8, P).T
    cpk[:, CP_GPLE:CP_GPLE + 8] = f32(inp["norm_ple_g"][l]).reshape(8, P).T
    cpk[:, CP_PSC:CP_PSC + 4] = f32(inp["pool_scale"][l]).reshape(4, P).T
    cw = f32(inp["conv_w"][l]).reshape(3, 4, P)
    cpk[:, CP_CONV:CP_CONV + 12] = cw.transpose(2, 0, 1).reshape(P, 12)
    cpk[:, CP_GQ] = np.tile(f32(inp["q_norm_g"][l]), 2)
    cpk[:, CP_GK] = np.tile(f32(inp["k_norm_g"][l]), 2)
    for k, nm in enumerate(("lam_q1", "lam_k1", "lam_q2", "lam_k2")):
        cpk[:, CP_LAM + k * 64:CP_LAM + (k + 1) * 64] = np.broadcast_to(f32(inp[nm][l])[None, :], (P, 64))
    cpk[:, CP_GSUB:CP_GSUB + P] = np.broadcast_to(f32(inp["sub_norm_g"][l])[None, :], (P, P))
    cpk[:, CP_POOLW:CP_POOLW + 4 * P] = f32(inp["pool_w"][l]).transpose(1, 0, 2).reshape(P, 4 * P)
    li0 = lam_init_of(l)
    cpk[:, CP_LI] = -li0
    cpk[:, CP_LI + 1] = 1.0 - li0
    return cpk


def make_kpack(r):
    kpk = np.zeros((P, KP), np.float32)
    kpk[:, KP_PAR] = float(r)
    kpk[:, KP_NPAR] = float(1 - r)
    for g, win in enumerate((2, 4, 8, 16)):
        for t in range(HALO):
            kpk[:, KP_CNT + g * HALO + t] = (1.0 / min(t + 1, win)) if r == 0 else (1.0 / win)
    low = (np.arange(P) < 64).astype(np.float32)
    diag = [np.zeros(P, np.float32), low, np.ones(P, np.float32)]
    zero = [np.zeros(P, np.float32)] * 3
    one = [np.ones(P, np.float32)] * 3
    tabs = (diag, zero) if r == 0 else (one, diag)
    for s in range(2):
        for ri in range(3):
            kpk[:, KP_MC + s * 3 + ri] = tabs[s][ri]
    kpk[:, KP_ID:KP_ID + P] = np.eye(P, dtype=np.float32)
    return kpk


_PROG_CACHE = {}


def get_prog(NT, layers_n, mode):
    key = (NT, layers_n, mode)
    if key not in _PROG_CACHE:
        _PROG_CACHE[key] = build_program({"NT": NT, "layers": list(range(layers_n)) if mode == "F" else [0], "mode": mode})[0]
    return _PROG_CACHE[key]


WNAMES2 = ("w_pool_out", "w_conv_out", "w_attn_out", "w_o", "w_up", "w_down", "w_ple_gate", "w_ple_proj")


def run_unfused(inp, NT, B, depth):
    ncores = 2 * B
    x = np.asarray(inp["x"], np.float32)
    p = np.asarray(inp["p"], np.float32)
    idxs = [tok_index(NT, r) for r in range(2)]
    hT = [np.ascontiguousarray(x[cid // 2][idxs[cid % 2]].T) for cid in range(ncores)]
    kpk = [make_kpack(cid % 2) for cid in range(ncores)]
    for l in range(depth):
        cpk = make_cpack(inp, l)[None]
        w_in = np.ascontiguousarray(np.asarray(inp["w_in"], np.float32)[l][None])
        ncK = get_prog_l(NT, 0, "K")
        maps = [{"hT_in": hT[cid], "w_in": w_in, "cpack": cpk, "kpack": kpk[cid]} for cid in range(ncores)]
        res = run_bass_kernel_spmd(ncK, maps, core_ids=list(range(ncores)))
        xnames = ["xk0_%d" % i for i in range(NT)] + ["xv0_%d" % i for i in range(NT)] + ["xh0"]
        xbs = [{nm: np.asarray(r[nm]) for nm in xnames} for r in res.results]
        nc2 = get_prog_l(NT, 0, "2")
        maps = []
        for cid in range(ncores):
            b = cid // 2
            m = {"hT_in": hT[cid], "w_in": w_in, "cpack": cpk, "kpack": kpk[cid],
                 "pT": np.ascontiguousarray(p[l, b][idxs[cid % 2]].T)[None]}
            for nm in xnames:
                m[nm.replace("xk", "xka").replace("xv", "xva").replace("xh", "xha")] = np.ascontiguousarray(
                    np.concatenate([xbs[2 * b][nm], xbs[2 * b + 1][nm]], axis=0))
            for nm in WNAMES2:
                m[nm] = np.ascontiguousarray(np.asarray(inp[nm], np.float32)[l][None])
            maps.append(m)
        res = run_bass_kernel_spmd(nc2, maps, core_ids=list(range(ncores)))
        hT = [np.asarray(r["hT_out"]) for r in res.results]
    out = np.zeros((B, 2 * NT * T, D), np.float32)
    for cid in range(ncores):
        out[cid // 2][idxs[cid % 2]] = hT[cid].T
    return out


def get_prog_l(NT, layer, mode):
    key = (NT, layer, mode)
    if key not in _PROG_CACHE:
        _PROG_CACHE[key] = build_program({"NT": NT, "layers": [layer], "mode": mode})[0]
    return _PROG_CACHE[key]


def run_fused(inp, NT, B, depth):
    ncores = 2 * B
    x = np.asarray(inp["x"], np.float32)
    p = np.asarray(inp["p"], np.float32)
    idxs = [tok_index(NT, r) for r in range(2)]
    key = (NT, depth, "F")
    if key not in _PROG_CACHE:
        _PROG_CACHE[key] = build_program({"NT": NT, "layers": list(range(depth)), "mode": "F", "B": B, "fake_cc": FAKE_CC})[0]
    nc = _PROG_CACHE[key]
    cpk = np.stack([make_cpack(inp, l) for l in range(depth)])
    shared = {"w_in": np.ascontiguousarray(np.asarray(inp["w_in"], np.float32)[:depth]), "cpack": cpk}
    for nm in WNAMES2:
        shared[nm] = np.ascontiguousarray(np.asarray(inp[nm], np.float32)[:depth])
    maps = []
    for cid in range(ncores):
        b, r = cid // 2, cid % 2
        m = dict(shared)
        m["hT_in"] = np.ascontiguousarray(x[b][idxs[r]].T)
        m["pT"] = np.ascontiguousarray(p[:depth, b][:, idxs[r]].transpose(0, 2, 1))
        m["kpack"] = make_kpack(r)
        maps.append(m)
    res = run_bass_kernel_spmd(nc, maps, core_ids=list(range(ncores)))
    out = np.zeros((B, 2 * NT * T, D), np.float32)
    for cid in range(ncores):
        out[cid // 2][idxs[cid % 2]] = np.asarray(res.results[cid]["hT_out"]).T
    return out


FUSED = True
FAKE_CC = False


def kernel(**inputs):
    if FUSED:
        return run_fused(inputs, 8, 4, 2)
    return run_unfused(inputs, 8, 4, 2)
```
